# Optimizing a Trainium2 kernel written in Bass

```python
import jax, jax.numpy as jnp
from jax import lax
import numpy as np


D_MODEL = 2048
BATCH = 4
SEQ = 4096
DEPTH = 1
DEC_BATCH = 1
DEC_SEQ = 16384
PAST_LEN = 128

GRID_W = 64
HEAD_DIM = 128
N_Q_HEADS = 8
N_KV_HEADS = 2
ATTN_WIDTH = N_Q_HEADS * HEAD_DIM
KV_WIDTH = N_KV_HEADS * HEAD_DIM
ROPE_THETA = 10000.0
Q_BLOCK = 128
GLA_HEADS = 4
GLA_DK = 128
GLA_DV = 256
GLA_K_WIDTH = GLA_HEADS * GLA_DK
GLA_V_WIDTH = GLA_HEADS * GLA_DV
GLA_GATE_RANK = 16
GLA_GATE_NORMALIZER = 16.0
GLA_CHUNK = 64
N_BRANCHES = 2
D_FF = 4 * D_MODEL
NORM_EPS = 1e-6

IN_SPLITS = (ATTN_WIDTH, KV_WIDTH, KV_WIDTH,
             GLA_K_WIDTH, GLA_K_WIDTH, GLA_V_WIDTH, GLA_V_WIDTH,
             GLA_GATE_RANK, GLA_GATE_RANK,
             N_BRANCHES * D_MODEL)
D_IN_PROJ = sum(IN_SPLITS)

kernel_name = 'hybrid_gqa_gla_gated_encoder'


def rms_norm(x, gain):
    xf = x.astype(jnp.float32)
    y = xf * lax.rsqrt(jnp.mean(xf * xf, axis=-1, keepdims=True) + NORM_EPS)
    return (y * gain.astype(jnp.float32)).astype(x.dtype)


def axial_rope_tables(T):
    n_rows = T // GRID_W
    rows = jnp.repeat(jnp.arange(n_rows, dtype=jnp.float32), GRID_W)
    cols = jnp.tile(jnp.arange(GRID_W, dtype=jnp.float32), n_rows)
    sec = HEAD_DIM // 2
    inv_freq = ROPE_THETA ** (-jnp.arange(0, sec, 2, dtype=jnp.float32) / sec)
    ang_r = rows[:, None] * inv_freq[None, :]
    ang_c = cols[:, None] * inv_freq[None, :]
    return jnp.cos(ang_r), jnp.sin(ang_r), jnp.cos(ang_c), jnp.sin(ang_c)


def rotate_section(x, cos, sin):
    h = x.shape[-1] // 2
    x1, x2 = x[..., :h], x[..., h:]
    c = cos[None, :, None, :]
    s = sin[None, :, None, :]
    return jnp.concatenate([x1 * c - x2 * s, x1 * s + x2 * c], axis=-1)


def apply_axial_rope(x, tables):
    cos_r, sin_r, cos_c, sin_c = tables
    sec = HEAD_DIM // 2
    xf = x.astype(jnp.float32)
    out = jnp.concatenate([rotate_section(xf[..., :sec], cos_r, sin_r),
                           rotate_section(xf[..., sec:], cos_c, sin_c)], axis=-1)
    return out


def gqa_attention(q, k, v, q_gain, k_gain):
    B, T = q.shape[0], q.shape[1]
    dt = v.dtype
    q = q.reshape(B, T, N_Q_HEADS, HEAD_DIM)
    k = k.reshape(B, T, N_KV_HEADS, HEAD_DIM)
    v = v.reshape(B, T, N_KV_HEADS, HEAD_DIM)
    tables = axial_rope_tables(T)
    q = (apply_axial_rope(rms_norm(q, q_gain), tables) * (HEAD_DIM ** -0.5)).astype(dt)
    k = apply_axial_rope(rms_norm(k, k_gain), tables).astype(dt)
    G = N_Q_HEADS // N_KV_HEADS
    nblk = T // Q_BLOCK
    qb = q.reshape(B, nblk, Q_BLOCK, N_KV_HEADS, G, HEAD_DIM).transpose(1, 0, 3, 4, 2, 5)
    kt = k.transpose(0, 2, 1, 3)
    vt = v.transpose(0, 2, 1, 3)

    def block(qi):
        s = jnp.einsum('bkgqd,bktd->bkgqt', qi, kt, preferred_element_type=jnp.float32)
        p = jax.nn.softmax(s, axis=-1)
        return jnp.einsum('bkgqt,bktd->bkgqd', p.astype(dt), vt)

    o = lax.map(block, qb)
    return o.transpose(1, 0, 4, 2, 3, 5).reshape(B, T, ATTN_WIDTH)


def gla_scan(q, k, v, log_a, strict):
    B, T, H, dk = q.shape
    dv = v.shape[-1]
    C = GLA_CHUNK
    nc = T // C

    def chunks(t):
        return t.reshape(B, nc, C, H, t.shape[-1]).transpose(1, 0, 3, 2, 4)

    q, k, v, log_a = chunks(q), chunks(k), chunks(v), chunks(log_a)
    b = jnp.cumsum(log_a, axis=3)
    b_last = b[:, :, :, -1:, :]
    qe = q * jnp.exp(b)
    ke = k * jnp.exp(-b)
    kd = k * jnp.exp(b_last - b)
    mask = jnp.tril(jnp.ones((C, C), dtype=bool), -1 if strict else 0)
    A = jnp.where(mask, jnp.einsum('nbhid,nbhjd->nbhij', qe, ke), 0.0)
    o_intra = jnp.einsum('nbhij,nbhjv->nbhiv', A, v)
    decay = jnp.exp(b_last)

    def step(S, inp):
        qe_n, kd_n, v_n, decay_n = inp
        o = jnp.einsum('bhid,bhdv->bhiv', qe_n, S)
        S = decay_n[..., 0, :, None] * S + jnp.einsum('bhjd,bhjv->bhdv', kd_n, v_n)
        return S, o

    S0 = jnp.zeros((B, H, dk, dv), jnp.float32)
    _, o_inter = lax.scan(step, S0, (qe, kd, v, decay))
    o = o_intra + o_inter
    return o.transpose(1, 0, 3, 2, 4).reshape(B, T, H, dv)


def gla_branch(q, k, v, g, lr_f, lr_b, w_up_f, b_f, w_up_b, b_b, norm_gain):
    B, T = q.shape[0], q.shape[1]
    dt = v.dtype
    f32 = jnp.float32
    qf = q.astype(f32).reshape(B, T, GLA_HEADS, GLA_DK) * (GLA_DK ** -0.5)
    kf = k.astype(f32).reshape(B, T, GLA_HEADS, GLA_DK)
    vf = v.astype(f32).reshape(B, T, GLA_HEADS, GLA_DV)
    la_f = (jax.nn.log_sigmoid((lr_f @ w_up_f + b_f).astype(f32)) / GLA_GATE_NORMALIZER
            ).reshape(B, T, GLA_HEADS, GLA_DK)
    la_b = (jax.nn.log_sigmoid((lr_b @ w_up_b + b_b).astype(f32)) / GLA_GATE_NORMALIZER
            ).reshape(B, T, GLA_HEADS, GLA_DK)
    o_f = gla_scan(qf, kf, vf, la_f, False)
    flip = lambda t: jnp.flip(t, axis=1)
    o_b = flip(gla_scan(flip(qf), flip(kf), flip(vf), flip(la_b), True))
    o = rms_norm(o_f + o_b, norm_gain)
    o = o.reshape(B, T, GLA_V_WIDTH) * jax.nn.silu(g.astype(f32))
    return o.astype(dt)


def encoder_layer(x, norm_mix, w_in, q_norm, k_norm, w_gate_up_fwd, b_gate_fwd,
                  w_gate_up_bwd, b_gate_bwd, gla_norm, w_attn_proj, w_gla_proj, b_merge,
                  w_out, norm_mlp, w_up, w_down):
    xn = rms_norm(x, norm_mix)
    proj = xn @ w_in
    offsets = np.cumsum(np.array(IN_SPLITS))[:-1].tolist()
    (q_a, k_a, v_a, q_g, k_g, v_g, g_g, lr_f, lr_b, gate_logits) = jnp.split(proj, offsets, axis=-1)
    a = gqa_attention(q_a, k_a, v_a, q_norm, k_norm) @ w_attn_proj
    b = gla_branch(q_g, k_g, v_g, g_g, lr_f, lr_b, w_gate_up_fwd, b_gate_fwd,
                   w_gate_up_bwd, b_gate_bwd, gla_norm) @ w_gla_proj
    gates = jax.nn.sigmoid((gate_logits + b_merge).astype(jnp.float32))
    g_a, g_b = gates[..., :D_MODEL], gates[..., D_MODEL:]
    mixed = (g_a * a.astype(jnp.float32) + g_b * b.astype(jnp.float32)).astype(x.dtype)
    h = x + mixed @ w_out
    hn = rms_norm(h, norm_mlp)
    u = jnp.square(jax.nn.relu(hn @ w_up))
    return h + u @ w_down


def trunk(x, norm_mix, w_in, q_norm, k_norm, w_gate_up_fwd, b_gate_fwd, w_gate_up_bwd,
          b_gate_bwd, gla_norm, w_attn_proj, w_gla_proj, b_merge, w_out, norm_mlp, w_up,
          w_down, norm_final):
    for l in range(DEPTH):
        x = encoder_layer(x, norm_mix[l], w_in[l], q_norm[l], k_norm[l], w_gate_up_fwd[l],
                          b_gate_fwd[l], w_gate_up_bwd[l], b_gate_bwd[l], gla_norm[l],
                          w_attn_proj[l], w_gla_proj[l], b_merge[l], w_out[l], norm_mlp[l],
                          w_up[l], w_down[l])
    return rms_norm(x, norm_final)


def setup_inputs(seed: int = 0) -> dict:
    key = jax.random.key(seed)
    ks = jax.random.split(key, 20)
    f32 = jnp.float32
    nrm = lambda k, shape, scale: jax.random.normal(k, shape, f32) * scale
    gain = lambda k, shape: 1.0 + 0.02 * jax.random.normal(k, shape, f32)
    L = DEPTH
    return {
        'x_prompt': jax.random.normal(ks[0], (BATCH, SEQ, D_MODEL), f32),
        'x_sample': jax.random.normal(ks[1], (DEC_BATCH, DEC_SEQ, D_MODEL), f32),
        'norm_mix': gain(ks[2], (L, D_MODEL)),
        'w_in': nrm(ks[3], (L, D_MODEL, D_IN_PROJ), D_MODEL ** -0.5),
        'q_norm': gain(ks[4], (L, HEAD_DIM)),
        'k_norm': gain(ks[5], (L, HEAD_DIM)),
        'w_gate_up_fwd': nrm(ks[6], (L, GLA_GATE_RANK, GLA_K_WIDTH), GLA_GATE_RANK ** -0.5),
        'b_gate_fwd': nrm(ks[7], (L, GLA_K_WIDTH), 0.02),
        'w_gate_up_bwd': nrm(ks[8], (L, GLA_GATE_RANK, GLA_K_WIDTH), GLA_GATE_RANK ** -0.5),
        'b_gate_bwd': nrm(ks[9], (L, GLA_K_WIDTH), 0.02),
        'gla_norm': gain(ks[10], (L, GLA_DV)),
        'w_attn_proj': nrm(ks[11], (L, ATTN_WIDTH, D_MODEL), ATTN_WIDTH ** -0.5),
        'w_gla_proj': nrm(ks[12], (L, GLA_V_WIDTH, D_MODEL), GLA_V_WIDTH ** -0.5),
        'b_merge': nrm(ks[13], (L, N_BRANCHES * D_MODEL), 0.02),
        'w_out': nrm(ks[14], (L, D_MODEL, D_MODEL), D_MODEL ** -0.5),
        'norm_mlp': gain(ks[15], (L, D_MODEL)),
        'w_up': nrm(ks[16], (L, D_MODEL, D_FF), D_MODEL ** -0.5),
        'w_down': nrm(ks[17], (L, D_FF, D_MODEL), D_FF ** -0.5),
        'norm_final': gain(ks[18], (D_MODEL,)),
    }


def reference(x_prompt, x_sample, norm_mix, w_in, q_norm, k_norm, w_gate_up_fwd, b_gate_fwd,
              w_gate_up_bwd, b_gate_bwd, gla_norm, w_attn_proj, w_gla_proj, b_merge, w_out,
              norm_mlp, w_up, w_down, norm_final):
    y_prompt = trunk(x_prompt, norm_mix, w_in, q_norm, k_norm, w_gate_up_fwd, b_gate_fwd,
                     w_gate_up_bwd, b_gate_bwd, gla_norm, w_attn_proj, w_gla_proj, b_merge,
                     w_out, norm_mlp, w_up, w_down, norm_final)
    y_sample = trunk(x_sample, norm_mix, w_in, q_norm, k_norm, w_gate_up_fwd, b_gate_fwd,
                     w_gate_up_bwd, b_gate_bwd, gla_norm, w_attn_proj, w_gla_proj, b_merge,
                     w_out, norm_mlp, w_up, w_down, norm_final)
    return (y_prompt, y_sample)
```

```python
import numpy as np
import concourse.bass as bass
import concourse.mybir as mybir
from concourse.bass_utils import run_bass_kernel_spmd

F32 = mybir.dt.float32
BF16 = mybir.dt.bfloat16
AF = mybir.ActivationFunctionType
ALU = mybir.AluOpType


class Trk:
    __slots__ = ("w", "rc", "rd")

    def __init__(self):
        self.w = None
        self.rc = {}
        self.rd = []


class Ins:
    __slots__ = ("fn", "waits", "inc", "dsem", "dcount", "tag")

    def __init__(self, fn):
        self.fn = fn
        self.tag = ""
        self.waits = []
        self.inc = False
        self.dsem = None
        self.dcount = 0


class EngQ:
    def __init__(self, name):
        self.name = name
        self.ins = []
        self.seen_c = {}
        self.seen_d = set()
        self.sem = None


class Sched:
    ENGS = ("pe", "act", "dve", "pool", "sp")

    def __init__(self, nc, n_dsem=40):
        self.nc = nc
        self.q = {n: EngQ(n) for n in self.ENGS}
        self.n_dsem = n_dsem
        self.dsem_last = [None] * n_dsem
        self.dsem_cnt = [0] * n_dsem
        self.dsem_next = 0
        self.out_dmas = []
        self.tag = ""
        self.names = {}

    def _deps(self, eng, rd, wr, same_ok):
        dc = {}
        dd = []

        def add(dep):
            if dep is None:
                return
            if dep[0] == 'c':
                if dc.get(dep[1], -1) < dep[2]:
                    dc[dep[1]] = dep[2]
            else:
                dd.append(dep[1])

        for t in rd:
            add(t.w)
        for t in wr:
            add(t.w)
            for en, idx in t.rc.items():
                add(('c', en, idx))
            for d in t.rd:
                add(('d', d))
        waits = []
        for en, idx in dc.items():
            if en == eng.name and same_ok:
                continue
            if eng.seen_c.get(en, -1) >= idx:
                continue
            eng.seen_c[en] = idx
            self.q[en].ins[idx].inc = True
            waits.append(('c', en, idx))
        for d in dd:
            if id(d) in eng.seen_d:
                continue
            eng.seen_d.add(id(d))
            waits.append(('d', d))
        return waits

    def op(self, en, fn, rd=(), wr=(), same_ok=False):
        eng = self.q[en]
        ins = Ins(fn)
        ins.tag = self.tag
        ins.waits = self._deps(eng, rd, wr, same_ok)
        eng.ins.append(ins)
        me = ('c', en, len(eng.ins) - 1)
        for t in rd:
            if t.rc.get(en, -1) < me[2]:
                t.rc[en] = me[2]
        for t in wr:
            t.w = me
            t.rc = {}
            t.rd = []
        return ins

    def dma(self, en, fn, rd=(), wr=(), is_out=False):
        eng = self.q[en]
        ins = Ins(fn)
        ins.tag = self.tag
        ins.waits = self._deps(eng, rd, wr, False)
        k = self.dsem_next
        self.dsem_next = (k + 1) % self.n_dsem
        prev = self.dsem_last[k]
        if prev is not None and id(prev) not in eng.seen_d:
            eng.seen_d.add(id(prev))
            ins.waits.append(('d', prev))
        self.dsem_cnt[k] += 16
        ins.dsem = k
        ins.dcount = self.dsem_cnt[k]
        self.dsem_last[k] = ins
        eng.ins.append(ins)
        me = ('d', ins)
        for t in rd:
            t.rd.append(ins)
        for t in wr:
            t.w = me
            t.rc = {}
            t.rd = []
        if is_out:
            self.out_dmas.append(ins)
        return ins

    def barrier(self):
        last = {n: len(self.q[n].ins) - 1 for n in self.ENGS}
        dmas = [d for d in self.dsem_last if d is not None]
        for n in self.ENGS:
            eng = self.q[n]
            ins = Ins(None)
            for m in self.ENGS:
                if m == n or last[m] < 0:
                    continue
                idx = last[m]
                while idx >= 0 and self.q[m].ins[idx].dsem is not None:
                    idx -= 1
                while idx >= 0 and self.q[m].ins[idx].fn is None:
                    idx -= 1
                if idx < 0 or eng.seen_c.get(m, -1) >= idx:
                    continue
                eng.seen_c[m] = idx
                self.q[m].ins[idx].inc = True
                ins.waits.append(('c', m, idx))
            for d in dmas:
                if id(d) in eng.seen_d:
                    continue
                eng.seen_d.add(id(d))
                ins.waits.append(('d', d))
            eng.ins.append(ins)

    def emit(self, sems, dsems):
        nc = self.nc
        for n in self.ENGS:
            self.q[n].sem = sems[n]
        counts = {}
        for n in self.ENGS:
            c = 0
            arr = []
            for ins in self.q[n].ins:
                if ins.inc and ins.dsem is None and ins.fn is not None:
                    c += 1
                arr.append(c)
            counts[n] = arr

        def run(n, e):
            eng = self.q[n]
            for ins in eng.ins:
                for w in ins.waits:
                    if w[0] == 'c':
                        e.wait_ge(sems[w[1]], counts[w[1]][w[2]])
                    else:
                        e.wait_ge(dsems[w[1].dsem], w[1].dcount)
                if ins.fn is None:
                    continue
                r = ins.fn(e)
                try:
                    self.names[r.ins.name] = ins.tag
                except Exception:
                    pass
                if ins.dsem is not None:
                    r.then_inc(dsems[ins.dsem], 16)
                elif ins.inc:
                    r.then_inc(sems[n], 1)
            if n == "sp":
                for d in self.out_dmas:
                    e.wait_ge(dsems[d.dsem], d.dcount)

        with nc.Block() as block:
            @block.tensor
            def _(e):
                run("pe", e)

            @block.scalar
            def _(e):
                run("act", e)

            @block.vector
            def _(e):
                run("dve", e)

            @block.gpsimd
            def _(e):
                run("pool", e)

            @block.sync
            def _(e):
                run("sp", e)


class T:
    def __init__(self, ap, ncell=1):
        self.ap = ap
        self.c = [Trk() for _ in range(ncell)]

    def __getitem__(self, i):
        return self.c[i]

    @property
    def all(self):
        return self.c


D = 2048
KC = 16
DIN = 8736
DFF = 8192
C_QA, C_KA, C_VA, C_QG, C_KG, C_VG, C_GG, C_LRF, C_LRB, C_GA, C_GB = (
    0, 1024, 1280, 1536, 2048, 2560, 3584, 4608, 4624, 4640, 6688)
EPS = 1e-6
NSEG_P, NSEG_S = 2, 8
MT = 512
OPT_B1 = False
OPT_B2 = True
OPT_C = False
SUB = MT // 128


class Cfg:
    def __init__(self, TP=4096, TS=16384):
        self.TP, self.TS = TP, TS
        self.OP, self.OS = TP // NSEG_P, TS // NSEG_S
        self.OWN = self.OP + self.OS


CI_PERM, CI_LF, CI_LB, CI_UF, CI_UB, CI_MF, CI_MB, CI_ONES = [i * 128 for i in range(8)]
NCONST = 8 * 128
VI_GMIX, VI_GMLP, VI_GFIN = 0, 16, 32
VI_GQ, VI_GK = 48, 49
VI_GN = 50
VI_BMA, VI_BMB = 52, 68
VI_FLAG = 84
VI_NFLAG = 94
VI_NEG16 = 104
VI_ONE = 106
NV = 108


def host_consts():
    c = np.zeros((128, NCONST), np.float32)
    j = np.arange(128)[:, None]
    i = np.arange(128)[None, :]
    pm = np.zeros((128, 128), np.float32)
    for m in range(128):
        sec = (m // 64) * 64
        r = m - sec
        if r < 32:
            pm[m, sec + r + 32] = -1.0
        else:
            pm[m, sec + r - 32] = 1.0
    c[:, CI_PERM:CI_PERM + 128] = pm.T
    c[:, CI_LF:CI_LF + 128] = (j <= i) * (-1.0 / 16)
    c[:, CI_LB:CI_LB + 128] = (j >= i) * (-1.0 / 16)
    c[:, CI_UF:CI_UF + 128] = (j > i) * (-1.0 / 16)
    c[:, CI_UB:CI_UB + 128] = (j < i) * (-1.0 / 16)
    c[:, CI_MF:CI_MF + 128] = (j <= i) * 1.0
    c[:, CI_MB:CI_MB + 128] = (j > i) * 1.0
    c[:, CI_ONES:CI_ONES + 128] = 1.0
    return c


def host_rope(T):
    t = np.arange(T, dtype=np.float32)
    rows = np.floor(t / 64).astype(np.float32)
    cols = (t - rows * 64).astype(np.float32)
    sec = 64
    inv = (np.float32(10000.0) ** (-np.arange(0, sec, 2, dtype=np.float32) / np.float32(sec))).astype(np.float32)
    ang_r = (rows[None, :] * inv[:, None]).astype(np.float32)
    ang_c = (cols[None, :] * inv[:, None]).astype(np.float32)
    cs = np.zeros((128, 2, T), np.float32)
    cs[0:32, 0] = np.cos(ang_r); cs[32:64, 0] = np.cos(ang_r)
    cs[64:96, 0] = np.cos(ang_c); cs[96:128, 0] = np.cos(ang_c)
    cs[0:32, 1] = np.sin(ang_r); cs[32:64, 1] = np.sin(ang_r)
    cs[64:96, 1] = np.sin(ang_c); cs[96:128, 1] = np.sin(ang_c)
    return cs


class SBAlloc:
    def __init__(self, nc, base, top):
        self.nc, self.cur, self.top, self.n = nc, base, top, 0
        self.peak = base

    def alloc(self, shape, dt, ncell=1):
        per = 1
        for s in shape[1:]:
            per *= s
        nbytes = per * (2 if dt == BF16 else 4)
        off = (self.cur + 63) // 64 * 64
        self.cur = off + nbytes
        self.peak = max(self.peak, self.cur)
        assert self.cur <= self.top, ("SBUF overflow", self.cur, self.top)
        h = self.nc.alloc_sbuf_tensor_at("sb%d" % self.n, list(shape), dt, offset=off)
        self.n += 1
        return T(h, ncell)

    def mark(self):
        return self.cur

    def reset(self, m):
        self.cur = m


def build(cfg, debug=False):
    nc = bass.Bass("TRN2", target_bir_lowering=False)
    S = Sched(nc, n_dsem=64)
    TP, TS, OP, OS, OWN = cfg.TP, cfg.TS, cfg.OP, cfg.OS, cfg.OWN
    NOWN_SUB = OWN // 128

    def din(name, shape, dt=F32):
        return nc.dram_tensor(name, list(shape), dt, kind="ExternalInput").ap()

    def dscr(name, shape, dt=BF16):
        return nc.dram_tensor(name, list(shape), dt, kind="Internal").ap()

    def dout(name, shape, dt=F32):
        return nc.dram_tensor(name, list(shape), dt, kind="ExternalOutput").ap()

    xo = din("xo", [D, OWN]); xp = din("xp", [D, TP]); xs = din("xs", [D, TS])
    cs_o = din("cs_o", [128, 2, OWN]); cs_cp = din("cs_cp", [128, 2, TP]); cs_cs = din("cs_cs", [128, 2, TS])
    consts_d = din("consts", [128, NCONST]); vecs_d = din("vecs", [128, NV])
    rows_d = din("rows", [1, 256]); wgu_d = din("wgu", [2, 33, 512])
    w_in = din("w_in", [D, DIN]); w_ap = din("w_ap", [1024, D]); w_gp = din("w_gp", [1024, D])
    w_out = din("w_out", [D, D]); w_up = din("w_up", [D, DFF]); w_down = din("w_down", [DFF, D])
    yT = dout("yT", [D, OWN])

    wb_in = dscr("wb_in", [D, DIN]); wb_ap = dscr("wb_ap", [1024, D]); wb_gp = dscr("wb_gp", [1024, D])
    wb_out = dscr("wb_out", [D, D]); wb_up = dscr("wb_up", [D, DFF]); wb_down = dscr("wb_down", [DFF, D])
    Kt = {"P": dscr("KtP", [2, 128, TP]), "S": dscr("KtS", [2, 128, TS])}
    Vs = {"P": dscr("VP", [TP, 256]), "S": dscr("VS", [TS, 256])}
    SbScr = dscr("SbScr", [NOWN_SUB, 128, 4, 256])
    dbg_outs = {}

    sems = {n: nc.alloc_semaphore("s_" + n) for n in Sched.ENGS}
    dsems = [nc.alloc_semaphore("d%d" % i) for i in range(64)]

    sb = SBAlloc(nc, 16512, 229344 - 2048)
    pbig = [nc.alloc_psum_tensor("ps%d" % i, [128, 1024], F32) for i in range(4)]
    PB = [T(pbig[i // 2][:, (i % 2) * 512:(i % 2 + 1) * 512]) for i in range(8)]
    rot = [0]

    def nb(allowed=range(6)):
        allowed = list(allowed)
        b = allowed[rot[0] % len(allowed)]
        rot[0] += 1
        return PB[b]

    def mm(out, lhsT, rhs, start, stop, rd, wr):
        S.op("pe", lambda e: e.matmul(out, lhsT=lhsT, rhs=rhs, start=start, stop=stop),
             rd=rd, wr=wr, same_ok=True)

    def act(out, in_, func, rd, wr, bias=None, scale=None):
        kw = {}
        if bias is not None:
            kw["bias"] = bias
        if scale is not None:
            kw["scale"] = scale
        S.op("act", lambda e: e.activation(out=out, in_=in_, func=func, **kw), rd=rd, wr=wr)

    def tt(en, out, a, b, op, rd, wr):
        S.op(en, lambda e: e.tensor_tensor(out=out, in0=a, in1=b, op=op), rd=rd, wr=wr)

    def ts(en, out, a, s1, op0, rd, wr, s2=None, op1=None):
        if op1 is None:
            S.op(en, lambda e: e.tensor_scalar(out=out, in0=a, scalar1=s1, scalar2=None, op0=op0), rd=rd, wr=wr)
        else:
            S.op(en, lambda e: e.tensor_scalar(out=out, in0=a, scalar1=s1, scalar2=s2, op0=op0, op1=op1),
                 rd=rd, wr=wr)

    def stt(out, a, sc, b, op0, op1, rd, wr):
        S.op("dve", lambda e: e.scalar_tensor_tensor(out=out, in0=a, scalar=sc, in1=b, op0=op0, op1=op1),
             rd=rd, wr=wr)

    def recip(out, in_, rd, wr):
        S.op("dve", lambda e: e.reciprocal(out=out, in_=in_), rd=rd, wr=wr)

    def cpy(en, out, in_, rd, wr):
        if en == "act":
            act(out, in_, AF.Copy, rd, wr)
        else:
            S.op(en, lambda e: e.tensor_copy(out=out, in_=in_), rd=rd, wr=wr)

    def ld(out, in_, rd=(), wr=()):
        S.dma("sp", lambda e: e.dma_start(out=out, in_=in_), rd=rd, wr=wr)

    def st(out, in_, rd=(), wr=(), is_out=False):
        S.dma("pool", lambda e: e.dma_start(out=out, in_=in_), rd=rd, wr=wr, is_out=is_out)

    def memset(en, ap, val, wr):
        S.op(en, lambda e: e.memset(ap, val), rd=(), wr=wr)

    cst = sb.alloc([128, NCONST], F32)
    vec = sb.alloc([128, NV], F32)
    wgu = sb.alloc([33, 2, 512], BF16)
    ones_bf = sb.alloc([128, 128], BF16)
    negb = sb.alloc([128, 2], F32)
    lrT = sb.alloc([64, 512], BF16)
    Sf = sb.alloc([128, 4, 256], F32, 4)
    Sfin = {"P": sb.alloc([128, 4, 256], F32, 4), "S": sb.alloc([128, 4, 256], F32, 4)}
    rstd_b = sb.alloc([128, 512], F32)
    rcol = sb.alloc([128, 8], F32)
    xg = sb.alloc([128, KC, 512], BF16, KC)
    xring = [sb.alloc([128, 512], F32) for _ in range(3)]
    sqring = [sb.alloc([128, 512], BF16) for _ in range(2)]
    tmpA = sb.alloc([128, 512], F32)
    ssacc = [sb.alloc([128, 512], F32) for _ in range(2)]
    nr_sets = [[sb.alloc([128, 512], F32) for _ in range(6)] + [sb.alloc([128, 512], BF16)]]
    gsc = sb.alloc([128, 2], F32)
    permG = sb.alloc([128, 2, 128], F32)
    msetup = sb.mark()
    rows = sb.alloc([1, 256], F32)
    wgu32 = sb.alloc([33, 2, 512], F32)

    ld(cst.ap[:], consts_d, wr=cst.all)
    ld(vec.ap[:], vecs_d, wr=vec.all)
    ld(rows.ap[:], rows_d, wr=rows.all)
    ld(wgu32.ap[:], wgu_d.rearrange("a r c -> r a c"), wr=wgu32.all)
    cpy("dve", wgu.ap[:], wgu32.ap[:], rd=wgu32.all, wr=wgu.all)
    cpy("dve", ones_bf.ap[:], cst.ap[:, CI_ONES:CI_ONES + 128], rd=cst.all, wr=ones_bf.all)
    memset("pool", lrT.ap[:], 1.0, wr=lrT.all)

    def C(i, n=128):
        return cst.ap[:, i:i + n]

    ts("dve", gsc.ap[:, 0:1], vec.ap[:, VI_GQ:VI_GQ + 1], 128.0 ** -0.5, ALU.mult, rd=vec.all, wr=gsc.all)
    cpy("dve", gsc.ap[:, 1:2], vec.ap[:, VI_GK:VI_GK + 1], rd=vec.all + gsc.all, wr=gsc.all)
    for i_ in range(2):
        ts("dve", permG.ap[:, i_, :], cst.ap[:, CI_PERM:CI_PERM + 128], gsc.ap[:, i_:i_ + 1], ALU.mult,
           rd=cst.all + gsc.all, wr=permG.all)

    def V(i, n=1):
        return vec.ap[:, i:i + n]

    mx = sb.alloc([1, 4], F32)
    S.op("dve", lambda e: e.tensor_reduce(out=mx.ap[:, 0:1], in_=rows.ap[:, 0:128], axis=mybir.AxisListType.X,
                                          op=ALU.max, apply_absolute_value=True), rd=rows.all, wr=mx.all)
    S.op("dve", lambda e: e.tensor_reduce(out=mx.ap[:, 1:2], in_=rows.ap[:, 128:256], axis=mybir.AxisListType.X,
                                          op=ALU.max, apply_absolute_value=True), rd=rows.all + mx.all, wr=mx.all)
    tt("dve", mx.ap[:, 2:3], mx.ap[:, 0:1], mx.ap[:, 1:2], ALU.mult, rd=mx.all, wr=mx.all)
    ts("dve", mx.ap[:, 2:4], mx.ap[:, 2:3].broadcast_to([1, 2]), -(128.0 ** 0.5), ALU.mult, rd=mx.all, wr=mx.all)
    pbk = nb()
    mm(pbk.ap[:, 0:2], cst.ap[0:1, CI_ONES:CI_ONES + 128], mx.ap[0:1, 2:4], True, True,
       rd=cst.all + mx.all, wr=pbk.all)
    cpy("dve", negb.ap[:], pbk.ap[:, 0:2], rd=pbk.all, wr=negb.all)
    S.barrier()
    sb.reset(msetup)

    S.tag = "P0"
    cin = [sb.alloc([128, 1024], F32) for _ in range(2)]
    cout = [sb.alloc([128, 1024], BF16) for _ in range(2)]
    ci = [0]

    cring = {"n": 2}

    def cast_block(src, dst, r0, r1, c0, c1):
        for r in range(r0, r1):
            for cc in range(c0, c1, 1024):
                cw = min(1024, c1 - cc)
                i = ci[0]
                ci[0] += 1
                a, b = cin[i % cring["n"]], cout[i % cring["n"]]
                tg = S.tag
                S.tag = "P0"
                ld(a.ap[:, :cw], src[r * 128:(r + 1) * 128, cc:cc + cw], wr=a.all)
                cpy((("act", "act", "pool") if OPT_C else ("act", "dve", "pool"))[i % 3], b.ap[:, :cw],
                    a.ap[:, :cw], rd=a.all, wr=b.all)
                st(dst[r * 128:(r + 1) * 128, cc:cc + cw], b.ap[:, :cw], rd=b.all)
                S.tag = tg
                yield

    def cast_rest():
        yield from cast_block(w_in, wb_in, 0, KC, 0, C_KA)
        yield from cast_block(w_in, wb_in, 0, KC, C_KA + 512, C_KG)
        yield from cast_block(w_in, wb_in, 0, KC, C_GG, C_LRF)
        yield from cast_block(w_in, wb_in, 0, KC, C_LRF + 32, DIN)
        yield from cast_block(w_ap, wb_ap, 0, 8, 0, D)
        yield from cast_block(w_gp, wb_gp, 0, 8, 0, D)
        yield from cast_block(w_out, wb_out, 0, KC, 0, D)
        yield from cast_block(w_up, wb_up, 0, KC, 0, DFF)
        yield from cast_block(w_down, wb_down, 0, 64, 0, D)

    mtmp = sb.mark()
    cin += [sb.alloc([128, 1024], F32) for _ in range(6)]
    cout += [sb.alloc([128, 1024], BF16) for _ in range(6)]
    cring["n"] = 8
    for _ in cast_block(w_in, wb_in, 0, KC, C_KA, C_KA + 512):
        pass
    for _ in cast_block(w_in, wb_in, 0, KC, C_KG, C_GG):
        pass
    for _ in cast_block(w_in, wb_in, 0, KC, C_LRF, C_LRF + 32):
        pass
    S.barrier()
    cring["n"] = 2
    sb.reset(mtmp)
    castg = cast_rest()
    n_cast_rest = (KC * (1 + 1 + 1 + 5)) + 16 + 16 + 2 * KC + KC * 8 + 128

    def pump_cast(n):
        for _ in range(n):
            next(castg, None)

    wv_in = wb_in.rearrange("(k p) c -> p k c", p=128)
    cur = {"xg": xg}

    def x_part1(src, t0, gcol, xg_t, sa, ring=None, pe_acc=None):
        sv = src.rearrange("(k p) t -> p k t", p=128)
        ring = ring or xring
        if len(ring) >= KC:
            for k in range(KC):
                ld(ring[k].ap[:], sv[:, k, t0:t0 + 512], wr=ring[k].all)
        for k in range(KC):
            xr = ring[k % len(ring)]
            sq = sqring[k % 2]
            if len(ring) < KC:
                ld(xr.ap[:], sv[:, k, t0:t0 + 512], wr=xr.all)
            act(sq.ap[:], xr.ap[:], AF.Square, rd=xr.all, wr=sq.all)
            ts("dve", xg_t.ap[:, k, :], xr.ap[:], V(gcol + k), ALU.mult, rd=xr.all + vec.all, wr=[xg_t[k]])
            if pe_acc is not None:
                if k >= 1:
                    sqp = sqring[(k - 1) % 2]
                    mm(pe_acc.ap[:], ones_bf.ap[:], sqp.ap[:], k == 1, False, rd=sqp.all + ones_bf.all,
                       wr=pe_acc.all)
            elif k == 0:
                cpy("dve" if OPT_B1 else "pool", sa.ap[:], sq.ap[:], rd=sq.all, wr=sa.all)
            else:
                tt("dve" if OPT_B1 else "pool", sa.ap[:], sa.ap[:], sq.ap[:], ALU.add, rd=sa.all + sq.all,
                   wr=sa.all)
            yield

    def x_part2(sa, ssb=None):
        if ssb is not None:
            sqp = sqring[(KC - 1) % 2]
            mm(ssb.ap[:], ones_bf.ap[:], sqp.ap[:], False, True, rd=sqp.all + ones_bf.all, wr=ssb.all)
        if ssb is None:
            ssb = nb()
            mm(ssb.ap[:], C(CI_ONES), sa.ap[:], True, True, rd=cst.all + sa.all, wr=ssb.all)
        mk_rstd(ssb, 1.0 / D, rstd_b)
        pc = nb()
        for s in range(SUB):
            mm(pc.ap[:, 2 * s:2 * s + 2], rstd_b.ap[0:1, s * 128:(s + 1) * 128], vec.ap[0:1, VI_ONE:VI_ONE + 2],
               True, True, rd=rstd_b.all + vec.all, wr=pc.all)
        cpy("dve", rcol.ap[:], pc.ap[:, 0:8], rd=pc.all, wr=rcol.all)

    def x_tile(src, t0, gcol, ring=None):
        ssb = PB[6] if OPT_B2 else None
        for _ in x_part1(src, t0, gcol, cur["xg"], ssacc[0], ring, pe_acc=ssb):
            pass
        x_part2(ssacc[0], ssb)

    def mk_rstd(ssb, inv_n, out_t, width=512):
        act(tmpA.ap[:, :width], ssb.ap[:, :width], AF.Ln, rd=ssb.all, wr=tmpA.all, scale=inv_n, bias=EPS)
        act(out_t.ap[:, :width], tmpA.ap[:, :width], AF.Exp, rd=tmpA.all, wr=out_t.all, scale=-0.5)


    def qk_nr1(ps, which, si):
        ty, ty2, trs, trr, t1, t2, tsq = nr_sets[si]
        tt("dve", ty.ap[:], ps.ap[:], rstd_b.ap[:], ALU.mult, rd=ps.all + rstd_b.all, wr=ty.all)
        act(tsq.ap[:], ty.ap[:], AF.Square, rd=ty.all, wr=tsq.all)

    def qk_nr2(which, si, cs_t, out_ap, out_trk):
        ty, ty2, trs, trr, t1, t2, tsq = nr_sets[si]
        pa, pb = nb(), nb()
        mm(pa.ap[:], ones_bf.ap[:], tsq.ap[:], True, True, rd=tsq.all + ones_bf.all, wr=pa.all)
        mm(pb.ap[:], permG.ap[:, which, :], ty.ap[:], True, True, rd=ty.all + permG.all, wr=pb.all)
        act(trs.ap[:], pa.ap[:], AF.Ln, rd=pa.all, wr=trs.all, scale=1.0 / 128, bias=EPS)
        act(trr.ap[:], trs.ap[:], AF.Exp, rd=trs.all, wr=trr.all, scale=-0.5)
        stt(t1.ap[:], ty.ap[:], gsc.ap[:, which:which + 1], cs_t.ap[:, 0, :], ALU.mult, ALU.mult,
            rd=ty.all + gsc.all + cs_t.all, wr=t1.all)
        tt("dve", t2.ap[:], pb.ap[:], cs_t.ap[:, 1, :], ALU.mult, rd=pb.all + cs_t.all, wr=t2.all)
        tt("pool", t1.ap[:], t1.ap[:], t2.ap[:], ALU.add, rd=t1.all + t2.all, wr=t1.all)
        tt("dve", out_ap, t1.ap[:], trr.ap[:], ALU.mult, rd=t1.all + trr.all, wr=out_trk)

    def lr_proj(wlr):
        xg_ = cur["xg"]
        pl = nb()
        for k in range(KC):
            mm(pl.ap[0:32, :], wlr.ap[:, k, 0:32], xg_.ap[:, k, :], k == 0, k == KC - 1,
               rd=wlr.all + [xg_[k]], wr=pl.all)
        tt("dve", lrT.ap[0:32, :], pl.ap[0:32, :], rstd_b.ap[0:32, :], ALU.mult,
           rd=pl.all + rstd_b.all, wr=lrT.all)

    def softplus_neg(d_, s, sp_t, e_t):
        pz = nb()
        mm(pz.ap[:], lrT.ap[0:33, s * 128:(s + 1) * 128], wgu.ap[0:33, d_, :], True, True,
           rd=lrT.all + wgu.all, wr=pz.all)
        act(e_t.ap[:], pz.ap[:], AF.Exp, rd=pz.all, wr=e_t.all, scale=-1.0)
        act(sp_t.ap[:], e_t.ap[:], AF.Ln, rd=e_t.all, wr=sp_t.all, bias=1.0)

    def tok_proj(w_t, s, cols, nb_allowed=range(6)):
        xg_ = cur["xg"]
        pk = nb(nb_allowed)
        n = cols.stop - cols.start
        for k in range(KC):
            mm(pk.ap[:, :n], xg_.ap[:, k, s * 128:(s + 1) * 128], w_t.ap[:, k, cols], k == 0, k == KC - 1,
               rd=[xg_[k]] + w_t.all, wr=pk.all)
        return pk

    S.tag = "P1"
    m1 = sb.mark()
    wkv = sb.alloc([128, KC, 512], BF16)
    wgk = sb.alloc([128, KC, 512], BF16)
    wgv = [sb.alloc([128, KC, 512], BF16) for _ in range(2)]
    wlr = sb.alloc([128, KC, 32], BF16)
    ld(wkv.ap[:], wv_in[:, :, C_KA:C_KA + 512], wr=wkv.all)
    ld(wgk.ap[:], wv_in[:, :, C_KG:C_KG + 512], wr=wgk.all)
    for i in range(2):
        ld(wgv[i].ap[:], wv_in[:, :, C_VG + 512 * i:C_VG + 512 * (i + 1)], wr=wgv[i].all)
    ld(wlr.ap[:], wv_in[:, :, C_LRF:C_LRF + 32], wr=wlr.all)
    xgB = sb.alloc([128, KC, 512], BF16, KC)
    xgs = [xg, xgB]
    Sbin = {"P": sb.alloc([128, 4, 256], F32, 4), "S": sb.alloc([128, 4, 256], F32, 4)}
    cs_t = sb.alloc([128, 2, 512], F32)
    kout = [sb.alloc([128, 512], BF16) for _ in range(2)]
    va = sb.alloc([128, SUB, 256], BF16)
    e_t = sb.alloc([128, 512], F32)
    sp = [[sb.alloc([128, 512], F32) for _ in range(2)] for _ in range(2)]
    ekd = [sb.alloc([128, 512], F32) for _ in range(2)]
    kg32 = [sb.alloc([128, 512], F32) for _ in range(2)]
    kd = [sb.alloc([128, 512], BF16) for _ in range(2)]
    vg = [sb.alloc([128, 1024], BF16) for _ in range(2)]
    dec = sb.alloc([128, 8], F32)
    Bcol = sb.alloc([128, 8], F32)
    eB = sb.alloc([128, 8], F32)
    totb = sb.alloc([128, 8], F32)
    sbf = sb.alloc([128, 4, 256], BF16, 4)

    def gla_kv(s, b):
        pk = tok_proj(wgk, s, slice(0, 512))
        act(kg32[b].ap[:], pk.ap[:], AF.Copy, rd=pk.all + rcol.all, wr=kg32[b].all,
            scale=rcol.ap[:, 2 * s:2 * s + 1])
        for i in range(2):
            pv = tok_proj(wgv[i], s, slice(0, 512))
            ts("dve", vg[b].ap[:, 512 * i:512 * (i + 1)], pv.ap[:], rcol.ap[:, 2 * s:2 * s + 1], ALU.mult,
               rd=pv.all + rcol.all, wr=vg[b].all)

    def kd_make(d_, b, umat_col, with_brow):
        pu = nb()
        mm(pu.ap[:], C(umat_col), sp[b][d_].ap[:], True, True, rd=cst.all + sp[b][d_].all, wr=pu.all)
        act(ekd[d_].ap[:], pu.ap[:], AF.Exp, rd=pu.all, wr=ekd[d_].all)
        tt("dve" if d_ == 0 else "pool", kd[d_].ap[:], kg32[b].ap[:], ekd[d_].ap[:], ALU.mult,
           rd=kg32[b].all + ekd[d_].all, wr=kd[d_].all)

    def decay_make(d_, b):
        pd = nb()
        for h in range(4):
            mm(pd.ap[:, 2 * h:2 * h + 2], sp[b][d_].ap[:, h * 128:(h + 1) * 128], V(VI_NEG16, 2), True, True,
               rd=sp[b][d_].all + vec.all, wr=pd.all)
        act(dec.ap[:], pd.ap[:, 0:8], AF.Exp, rd=pd.all, wr=dec.all)

    def state_U(d_, h, b):
        pU = nb()
        mm(pU.ap[:, 0:256], kd[d_].ap[:, h * 128:(h + 1) * 128], vg[b].ap[:, h * 256:(h + 1) * 256], True, True,
           rd=kd[d_].all + vg[b].all, wr=pU.all)
        return pU

    tot_tiles = (TP + TS) // 512
    cast_per_tile = -(-n_cast_rest // tot_tiles)
    for seq, src, T_, nseg, f0, cs_c in (("P", xp, TP, NSEG_P, 0, cs_cp), ("S", xs, TS, NSEG_S, NSEG_P, cs_cs)):
        seg_sub = (T_ // nseg) // 128
        nt = T_ // 512
        nt_gla = nt - (T_ // nseg) // 512
        Sacc, accf = Sbin[seq], Sfin[seq]
        for h in range(4):
            memset("pool", Sf.ap[:, h, :], 0.0, wr=[Sf[h]])
            memset("pool", Sacc.ap[:, h, :], 0.0, wr=[Sacc[h]])
            memset("pool", accf.ap[:, h, :], 0.0, wr=[accf[h]])
        memset("pool", Bcol.ap[:], 0.0, wr=Bcol.all)
        for _ in x_part1(src, 0, VI_GMIX, xgs[0], ssacc[0], pe_acc=PB[7]):
            pass
        for t in range(nt):
            t0 = t * 512
            cur["xg"] = xgs[t % 2]
            xg_ = cur["xg"]
            if t == 0:
                S.tag = 'P1.x2'
                x_part2(ssacc[t % 2], PB[7])
            S.tag = 'P1.kv'
            nxt = (x_part1(src, t0 + 512, VI_GMIX, xgs[(t + 1) % 2], ssacc[(t + 1) % 2], pe_acc=PB[7])
                   if t + 1 < nt else iter(()))

            def pump(n):
                for _ in range(n):
                    next(nxt, None)

            cq = [cast_per_tile]

            def tick():
                tg = S.tag
                S.tag = 'P1.x'
                next(nxt, None)
                S.tag = tg
                if cq[0] > 0:
                    cq[0] -= 1
                    pump_cast(1)
            ld(cs_t.ap[:], cs_c[:, :, t0:t0 + 512], wr=cs_t.all)
            def ka_proj(g):
                pk = nb()
                for k in range(KC):
                    mm(pk.ap[:], wkv.ap[:, k, g * 128:(g + 1) * 128], xg_.ap[:, k, :], k == 0, k == KC - 1,
                       rd=wkv.all + [xg_[k]], wr=pk.all)
                qk_nr1(pk, 1, 0)

            def va_proj(s):
                pv = tok_proj(wkv, s, slice(256, 512))
                act(va.ap[:, s, :], pv.ap[:, 0:256], AF.Copy, rd=pv.all + rcol.all, wr=va.all,
                    scale=rcol.ap[:, 2 * s:2 * s + 1])
            def kv_section():
                S.tag = 'P1.kv'
                ka_proj(0)
                tick()
                va_proj(0)
                tick()
                va_proj(1)
                tick()
                qk_nr2(1, 0, cs_t, kout[0].ap[:], kout[0].all)
                st(Kt[seq][0, :, t0:t0 + 512], kout[0].ap[:], rd=kout[0].all)
                tick()
                ka_proj(1)
                tick()
                va_proj(2)
                tick()
                va_proj(3)
                tick()
                qk_nr2(1, 0, cs_t, kout[1].ap[:], kout[1].all)
                st(Kt[seq][1, :, t0:t0 + 512], kout[1].ap[:], rd=kout[1].all)
                st(Vs[seq][t0:t0 + 512, :].rearrange("(s p) c -> p s c", p=128), va.ap[:], rd=va.all)
                tick()
            if t >= nt_gla:
                kv_section()
                pump(KC)
                if t + 1 < nt:
                    S.tag = 'P1.x2'
                    x_part2(ssacc[(t + 1) % 2], PB[7])
                pump_cast(cq[0])
                continue
            S.tag = 'P1.lr'
            lr_proj(wlr)
            tick()

            def stageA(s):
                b = s % 2
                S.tag = 'P1.A'
                gla_kv(s, b)
                softplus_neg(0, s, sp[b][0], e_t)
                softplus_neg(1, s, sp[b][1], e_t)

            def stageB1(s):
                b = s % 2
                S.tag = 'P1.B1'
                kd_make(0, b, CI_UF, False)
                kd_make(1, b, CI_UB, False)
                decay_make(0, b)
                act(eB.ap[:], Bcol.ap[:], AF.Exp, rd=Bcol.all, wr=eB.all)
                pdb = nb()
                for h in range(4):
                    mm(pdb.ap[:, 2 * h:2 * h + 2], sp[b][1].ap[:, h * 128:(h + 1) * 128], V(VI_NEG16, 2), True, True,
                       rd=sp[b][1].all + vec.all, wr=pdb.all)
                cpy("dve", totb.ap[:], pdb.ap[:, 0:8], rd=pdb.all, wr=totb.all)

            def stageB2(s):
                b = s % 2
                S.tag = 'P1.B2'
                gs = t * SUB + s
                seg = gs // seg_sub
                if gs % seg_sub == 0:
                    for h in range(4):
                        stt(accf.ap[:, h, :], Sf.ap[:, h, :], V(VI_FLAG + f0 + seg), accf.ap[:, h, :], ALU.mult,
                            ALU.add, rd=[Sf[h], accf[h]] + vec.all, wr=[accf[h]])
                for h in range(4):
                    pU = state_U(0, h, b)
                    stt(Sf.ap[:, h, :], Sf.ap[:, h, :], dec.ap[:, 2 * h:2 * h + 1], pU.ap[:, 0:256], ALU.mult,
                        ALU.add, rd=[Sf[h]] + dec.all + pU.all, wr=[Sf[h]])
                    pU = state_U(1, h, b)
                    stt(Sacc.ap[:, h, :], pU.ap[:, 0:256], eB.ap[:, 2 * h:2 * h + 1], Sacc.ap[:, h, :], ALU.mult,
                        ALU.add, rd=[Sacc[h]] + pU.all + eB.all, wr=[Sacc[h]])
                tt("dve", Bcol.ap[:], Bcol.ap[:], totb.ap[:], ALU.add, rd=Bcol.all + totb.all, wr=Bcol.all)
                if (gs + 1) % seg_sub == 0:
                    ts("dve", Bcol.ap[:], Bcol.ap[:], V(VI_NFLAG + f0 + seg + 1), ALU.mult,
                       rd=Bcol.all + vec.all, wr=Bcol.all)
                    for h in range(4):
                        ts("pool", Sacc.ap[:, h, :], Sacc.ap[:, h, :], V(VI_NFLAG + f0 + seg + 1), ALU.mult,
                           rd=[Sacc[h]] + vec.all, wr=[Sacc[h]], s2=1.0, op1=ALU.mult)

            stageA(0)
            tick()
            for s in range(SUB):
                stageB1(s)
                if s < SUB - 1:
                    tick()
                if s + 1 < SUB:
                    stageA(s + 1)
                    tick()
                if s == SUB - 1:
                    kv_section()
                    pump(KC)
                    if t + 1 < nt:
                        S.tag = 'P1.x2'
                        x_part2(ssacc[(t + 1) % 2], PB[7])
                stageB2(s)
                if s < SUB - 1:
                    tick()
            pump_cast(cq[0])
            if t == nt_gla - 1:
                for h in range(4):
                    stt(accf.ap[:, h, :], Sf.ap[:, h, :], V(VI_FLAG + f0 + nseg - 1), accf.ap[:, h, :], ALU.mult,
                        ALU.add, rd=[Sf[h], accf[h]] + vec.all, wr=[accf[h]])
    pump_cast(10000)

    S.tag = "P2"
    for seq, o0, olen in (("P", 0, OP), ("S", OP, OS)):
        for h in range(4):
            cpy("pool", Sf.ap[:, h, :], Sbin[seq].ap[:, h, :], rd=[Sbin[seq][h]], wr=[Sf[h]])
        tl = list(reversed(range(olen // 512)))
        for _ in x_part1(xo, o0 + tl[0] * 512, VI_GMIX, xgs[0], ssacc[0], pe_acc=PB[7]):
            pass
        for ti, t in enumerate(tl):
            t0 = o0 + t * 512
            cur["xg"] = xgs[ti % 2]
            x_part2(ssacc[ti % 2], PB[7])
            nxt = (x_part1(xo, o0 + tl[ti + 1] * 512, VI_GMIX, xgs[(ti + 1) % 2], ssacc[(ti + 1) % 2],
                           pe_acc=PB[7]) if ti + 1 < len(tl) else iter(()))
            lr_proj(wlr)
            sl = list(reversed(range(SUB)))

            def stageA2(s):
                b = s % 2
                gla_kv(s, b)
                softplus_neg(1, s, sp[b][1], e_t)

            stageA2(sl[0])
            for si, s in enumerate(sl):
                b = s % 2
                gs = (t0 // 128) + s
                kd_make(1, b, CI_UB, False)
                decay_make(1, b)
                if si + 1 < SUB:
                    stageA2(sl[si + 1])
                for h in range(4):
                    cpy("act" if h % 2 else "pool", sbf.ap[:, h, :], Sf.ap[:, h, :], rd=[Sf[h]], wr=[sbf[h]])
                st(SbScr[gs], sbf.ap[:], rd=sbf.all)
                for h in range(4):
                    pU = state_U(1, h, b)
                    stt(Sf.ap[:, h, :], Sf.ap[:, h, :], dec.ap[:, 2 * h:2 * h + 1], pU.ap[:, 0:256], ALU.mult,
                        ALU.add, rd=[Sf[h]] + dec.all + pU.all, wr=[Sf[h]])
                for _ in range(4):
                    next(nxt, None)
            for _ in nxt:
                pass
    cur["xg"] = xg
    S.barrier()
    sb.reset(m1)
    wv_ap = wb_ap.rearrange("(k p) c -> p k c", p=128)
    wv_gp = wb_gp.rearrange("(k p) c -> p k c", p=128)
    wv_out = wb_out.rearrange("(k p) c -> p k c", p=128)
    wv_up = wb_up.rearrange("(k p) c -> p k c", p=128)
    wv_dn = wb_down.rearrange("(k p) c -> p k c", p=128)
    wspec = {"qa0": (wv_in, 0, KC, C_QA, 512), "qa1": (wv_in, 0, KC, C_QA + 512, 512),
             "qg": (wv_in, 0, KC, C_QG, 512), "kg": (wv_in, 0, KC, C_KG, 512),
             "vg0": (wv_in, 0, KC, C_VG, 512), "vg1": (wv_in, 0, KC, C_VG + 512, 512),
             "gg0": (wv_in, 0, KC, C_GG, 512), "gg1": (wv_in, 0, KC, C_GG + 512, 512),
             "lr": (wv_in, 0, KC, C_LRF, 32)}
    wseq1 = ["qa0", "qa1", "qg", "kg", "lr", "vg0", "vg1", "gg0", "gg1"]
    for cg in range(4):
        wspec["ga%d" % cg] = (wv_in, 0, KC, C_GA + 512 * cg, 512)
        wspec["gb%d" % cg] = (wv_in, 0, KC, C_GB + 512 * cg, 512)
        wspec["ap%d" % cg] = (wv_ap, 0, 8, 512 * cg, 512)
        wspec["gp%d" % cg] = (wv_gp, 0, 8, 512 * cg, 512)
        wseq1 += ["ga%d" % cg, "gb%d" % cg, "ap%d" % cg, "gp%d" % cg]
    for cg in range(4):
        wspec["wo%d" % cg] = (wv_out, 0, KC, 512 * cg, 512)
        wseq1.append("wo%d" % cg)
    for G in range(4):
        for j in range(4):
            wspec["up%d_%d" % (G, j)] = (wv_up, 0, KC, G * 2048 + 512 * j, 512)
            wseq1.append("up%d_%d" % (G, j))
        for cg in range(4):
            wspec["dn%d_%d" % (G, cg)] = (wv_dn, G * KC, KC, 512 * cg, 512)
            wseq1.append("dn%d_%d" % (G, cg))
    NMT = OWN // MT
    wseq = wseq1 * NMT
    NW = 2
    wring = [sb.alloc([128, KC, 512], BF16) for _ in range(NW)]
    wstate = {"issued": 0, "pos": 0}

    def w_issue(upto):
        while wstate["issued"] <= min(upto, len(wseq) - 1):
            p = wstate["issued"]
            view, k0, nk, c0, ncol = wspec[wseq[p]]
            slot = wring[p % NW]
            ld(slot.ap[:, 0:nk, 0:ncol], view[:, k0:k0 + nk, c0:c0 + ncol], wr=slot.all)
            wstate["issued"] += 1

    def wnext(name):
        p = wstate["pos"]
        assert wseq[p] == name, (wseq[p], name)
        w_issue(p + NW - 1)
        wstate["pos"] += 1
        return wring[p % NW]

    cs_own = sb.alloc([128, 2, 512], F32)
    mf4 = sb.alloc([128, 4, 128], F32)
    mb4 = sb.alloc([128, 4, 128], F32)
    Sf_bf = sb.alloc([128, 4, 256], BF16, 4)
    for h in range(4):
        cpy("dve", mf4.ap[:, h, :], cst.ap[:, CI_MF:CI_MF + 128], rd=cst.all, wr=mf4.all)
        cpy("pool", mb4.ap[:, h, :], cst.ap[:, CI_MB:CI_MB + 128], rd=cst.all, wr=mb4.all)
    attnO = sb.alloc([128, 8, 512], BF16, 8)
    glaO = sb.alloc([128, 8, 512], BF16, 8)
    zm = sb.mark()
    xov = xo.rearrange("(k p) t -> p k t", p=128)
    PO2 = pbig[3]
    PO2_trk = PB[6].all + PB[7].all

    for m in range(NMT):
        tok0 = m * MT
        seq = "P" if tok0 < OP else "S"
        ctx = TP if seq == "P" else TS
        first_of_seg = (tok0 == 0) or (tok0 == OP)
        S.tag = "P3.x"
        sb.cur = zm + 36 * 1024
        bigring = xring + [sb.alloc([128, 512], F32) for _ in range(KC - 3)]
        x_tile(xo, tok0, VI_GMIX, bigring)
        ld(cs_own.ap[:], cs_o[:, :, tok0:tok0 + 512], wr=cs_own.all)

        S.tag = "P3.attnq"
        sb.reset(zm)
        qT = sb.alloc([128, 4, 512], BF16, 4)
        kring = [sb.alloc([128, 512], BF16) for _ in range(3)]
        vring = [sb.alloc([128, SUB, 128], BF16) for _ in range(3)]
        ptring = [sb.alloc([128, 512], BF16) for _ in range(4)]
        rinv = sb.alloc([128, 512], F32)
        sacc = [sb.alloc([128, 512], F32) for _ in range(2)]
        del nr_sets[1:]
        nr_sets.append([sb.alloc([128, 512], F32) for _ in range(6)] + [sb.alloc([128, 512], BF16)])
        for g in range(2):
            S.tag = "P3.attnq"
            w = wnext("qa%d" % g)
            for hh in range(4):
                pq = nb()
                for k in range(KC):
                    mm(pq.ap[:], w.ap[:, k, hh * 128:(hh + 1) * 128], xg.ap[:, k, :], k == 0, k == KC - 1,
                       rd=w.all + [xg[k]], wr=pq.all)
                qk_nr1(pq, 0, hh % 2)
                if hh >= 1:
                    qk_nr2(0, (hh - 1) % 2, cs_own, qT.ap[:, hh - 1, :], [qT[hh - 1]])
            qk_nr2(0, 1, cs_own, qT.ap[:, 3, :], [qT[3]])
            S.tag = "P3.attnq2"
            nkt = ctx // 512
            for pair in ((0, 1), (2, 3)):
                def kv_load(kt):
                    kr, vr = kring[kt % 3], vring[kt % 3]
                    ld(kr.ap[:], Kt[seq][g, :, kt * 512:(kt + 1) * 512], wr=kr.all)
                    ld(vr.ap[:], Vs[seq][kt * 512:(kt + 1) * 512, g * 128:(g + 1) * 128].rearrange(
                        "(s p) c -> p s c", p=128), wr=vr.all)
                S.tag = "P3.attn"
                kv_load(0)
                steps = [(kt, s) for kt in range(nkt) for s in range(SUB)]
                nst = len(steps)

                def qk(i):
                    kt, s = steps[i]
                    kr = kring[kt % 3]
                    for idx, hh in enumerate(pair):
                        bs = PB[2 * idx + (i % 2)]
                        mm(bs.ap[:], kr.ap[:, s * 128:(s + 1) * 128], qT.ap[:, hh, :], True, True,
                           rd=kr.all + [qT[hh]], wr=bs.all)

                qk(0)
                for i in range(nst):
                    kt, s = steps[i]
                    if s == 0 and kt + 1 < nkt:
                        kv_load(kt + 1)
                    if i + 1 < nst:
                        qk(i + 1)
                    vr = vring[kt % 3]
                    for idx, hh in enumerate(pair):
                        bs = PB[2 * idx + (i % 2)]
                        pt = ptring[(2 * i + idx) % 4]
                        act(pt.ap[:], bs.ap[:], AF.Exp, rd=bs.all + negb.all, wr=pt.all, bias=negb.ap[:, 0:1])
                        mm(PB[4 + idx].ap[:], vr.ap[:, s, :], pt.ap[:], i == 0, i == nst - 1,
                           rd=vr.all + pt.all, wr=PB[4 + idx].all)
                        if i == 0:
                            cpy("dve", sacc[idx].ap[:], pt.ap[:], rd=pt.all, wr=sacc[idx].all)
                        else:
                            tt("dve", sacc[idx].ap[:], sacc[idx].ap[:], pt.ap[:], ALU.add,
                               rd=sacc[idx].all + pt.all, wr=sacc[idx].all)
                S.tag = "P3.attnfin"
                for idx, hh in enumerate(pair):
                    mm(PB[6 + idx].ap[:], C(CI_ONES), sacc[idx].ap[:], True, True,
                       rd=cst.all + sacc[idx].all, wr=PB[6 + idx].all)
                    act(sacc[idx].ap[:], PB[6 + idx].ap[:], AF.Ln, rd=PB[6 + idx].all, wr=sacc[idx].all)
                    act(rinv.ap[:], sacc[idx].ap[:], AF.Exp, rd=sacc[idx].all, wr=rinv.all, scale=-1.0)
                    tt("dve", attnO.ap[:, g * 4 + hh, :], PB[4 + idx].ap[:], rinv.ap[:], ALU.mult,
                       rd=PB[4 + idx].all + rinv.all, wr=[attnO[g * 4 + hh]])
        S.barrier()

        S.tag = "P3.glaproj"
        sb.reset(zm)
        qgT = sb.alloc([128, 4, 512], BF16)
        kgT = sb.alloc([128, 4, 512], BF16)
        kgm = sb.alloc([128, SUB, 512], BF16, SUB)
        vgm = sb.alloc([128, SUB, 1024], BF16, SUB)
        sgn = sb.alloc([128, 8, 512], BF16)
        t8 = sb.alloc([128, 8, 128], F32)
        gt1 = T(t8.ap[:, 0:4, :].rearrange("p a t -> p (a t)"))
        gt1.c = t8.c
        gt2 = T(t8.ap[:, 4:8, :].rearrange("p a t -> p (a t)"))
        gt2.c = t8.c
        spm = [[sb.alloc([128, 512], F32) for _ in range(2)] for _ in range(2)]
        e_m = sb.alloc([128, 512], F32)
        Eq = [sb.alloc([128, 4, 128], F32) for _ in range(2)]
        Ek = [sb.alloc([128, 4, 128], F32) for _ in range(2)]
        qe = [sb.alloc([128, 4, 128], BF16) for _ in range(2)]
        ke = [sb.alloc([128, 4, 128], BF16) for _ in range(2)]
        Am = [sb.alloc([128, 4, 128], BF16) for _ in range(2)]
        ekdm = sb.alloc([128, 512], F32)
        kdm = sb.alloc([128, 512], BF16)
        sbsn = [sb.alloc([128, 4, 256], BF16) for _ in range(2)]
        osq = sb.alloc([128, 8, 128], BF16)
        rso = sb.alloc([128, 4, 128], F32)
        rro = sb.alloc([128, 4, 128], F32)

        w = wnext("qg")
        for h in range(4):
            pq = nb()
            for k in range(KC):
                mm(pq.ap[:], w.ap[:, k, h * 128:(h + 1) * 128], xg.ap[:, k, :], k == 0, k == KC - 1,
                   rd=w.all + [xg[k]], wr=pq.all)
            stt(qgT.ap[:, h, :], pq.ap[:], 128.0 ** -0.5, rstd_b.ap[:], ALU.mult, ALU.mult,
                rd=pq.all + rstd_b.all, wr=qgT.all)
        w = wnext("kg")
        for h in range(4):
            pq = nb()
            for k in range(KC):
                mm(pq.ap[:], w.ap[:, k, h * 128:(h + 1) * 128], xg.ap[:, k, :], k == 0, k == KC - 1,
                   rd=w.all + [xg[k]], wr=pq.all)
            tt("dve", kgT.ap[:, h, :], pq.ap[:], rstd_b.ap[:], ALU.mult, rd=pq.all + rstd_b.all, wr=kgT.all)
        for s in range(SUB):
            pk = tok_proj(w, s, slice(0, 512))
            act(kgm.ap[:, s, :], pk.ap[:], AF.Copy, rd=pk.all + rcol.all, wr=[kgm[s]],
                scale=rcol.ap[:, 2 * s:2 * s + 1])
        w = wnext("lr")
        lr_proj(w)
        softplus_neg(0, 0, spm[0][0], e_m)
        softplus_neg(1, 0, spm[0][1], e_m)
        for i in range(2):
            w = wnext("vg%d" % i)
            for s in range(SUB):
                pv = tok_proj(w, s, slice(0, 512))
                ts("dve", vgm.ap[:, s, 512 * i:512 * (i + 1)], pv.ap[:], rcol.ap[:, 2 * s:2 * s + 1], ALU.mult,
                   rd=pv.all + rcol.all, wr=[vgm[s]])
        for i in range(2):
            w = wnext("gg%d" % i)
            for c4 in range(4):
                ch = 4 * i + c4
                pg = nb()
                for k in range(KC):
                    mm(pg.ap[:], w.ap[:, k, c4 * 128:(c4 + 1) * 128], xg.ap[:, k, :], k == 0, k == KC - 1,
                       rd=w.all + [xg[k]], wr=pg.all)
                tt("dve", gt1.ap[:], pg.ap[:], rstd_b.ap[:], ALU.mult, rd=pg.all + rstd_b.all, wr=gt1.all)
                act(gt2.ap[:], gt1.ap[:], AF.Silu, rd=gt1.all, wr=gt2.all)
                ts("pool", sgn.ap[:, ch, :], gt2.ap[:], V(VI_GN + ch % 2), ALU.mult, rd=gt2.all + vec.all,
                   wr=sgn.all, s2=1.0, op1=ALU.mult)
        if first_of_seg:
            for h in range(4):
                cpy("pool", Sf.ap[:, h, :], Sfin[seq].ap[:, h, :], rd=[Sfin[seq][h]], wr=[Sf[h]])
                cpy("act", Sf_bf.ap[:, h, :], Sf.ap[:, h, :], rd=[Sf[h]], wr=[Sf_bf[h]])
        S.tag = "P3.gla"
        for s in range(SUB):
            gs = m * SUB + s
            sn = sbsn[s % 2]
            ld(sn.ap[:], SbScr[gs], wr=sn.all)
            spc = spm[s % 2]
            pbts = []
            for d_ in range(2):
                pbt = nb()
                for h in range(4):
                    mm(pbt.ap[:, h * 128:(h + 1) * 128], spc[d_].ap[:, h * 128:(h + 1) * 128],
                       C(CI_LF if d_ == 0 else CI_LB), True, True, rd=spc[d_].all + cst.all, wr=pbt.all)
                pbts.append(pbt)
            pu = nb()
            mm(pu.ap[:], C(CI_UF), spc[0].ap[:], True, True, rd=cst.all + spc[0].all, wr=pu.all)
            if s + 1 < SUB:
                softplus_neg(0, s + 1, spm[(s + 1) % 2][0], e_m)
                softplus_neg(1, s + 1, spm[(s + 1) % 2][1], e_m)
            for d_ in range(2):
                pbt = pbts[d_]
                pbv = pbt.ap[:].rearrange("p (h t) -> p h t", h=4)
                act(Eq[d_].ap[:], pbv, AF.Exp, rd=pbt.all, wr=Eq[d_].all)
                act(Ek[d_].ap[:], pbv, AF.Exp, rd=pbt.all, wr=Ek[d_].all, scale=-1.0)
                tt("dve", qe[d_].ap[:], qgT.ap[:, :, s * 128:(s + 1) * 128], Eq[d_].ap[:], ALU.mult,
                   rd=qgT.all + Eq[d_].all, wr=qe[d_].all)
                tt("pool", ke[d_].ap[:], kgT.ap[:, :, s * 128:(s + 1) * 128], Ek[d_].ap[:], ALU.mult,
                   rd=kgT.all + Ek[d_].all, wr=ke[d_].all)
            act(ekdm.ap[:], pu.ap[:], AF.Exp, rd=pu.all, wr=ekdm.all)
            tt("pool", kdm.ap[:], kgm.ap[:, s, :], ekdm.ap[:], ALU.mult, rd=[kgm[s]] + ekdm.all, wr=kdm.all)
            for d_ in range(2):
                pA = nb()
                for h in range(4):
                    mm(pA.ap[:, h * 128:(h + 1) * 128], ke[d_].ap[:, h, :], qe[d_].ap[:, h, :], True, True,
                       rd=ke[d_].all + qe[d_].all, wr=pA.all)
                msk = mf4 if d_ == 0 else mb4
                tt("dve", Am[d_].ap[:], pA.ap[:].rearrange("p (h t) -> p h t", h=4), msk.ap[:], ALU.mult,
                   rd=pA.all + msk.all, wr=Am[d_].all)
            for h in range(4):
                for c in range(2):
                    o_ap = PO2[:, (h * 2 + c) * 128:(h * 2 + c + 1) * 128]
                    vl = vgm.ap[:, s, h * 256 + c * 128:h * 256 + (c + 1) * 128]
                    mm(o_ap, vl, Am[0].ap[:, h, :], True, False, rd=[vgm[s]] + Am[0].all, wr=PO2_trk)
                    mm(o_ap, Sf_bf.ap[:, h, c * 128:(c + 1) * 128], qe[0].ap[:, h, :], False, False,
                       rd=[Sf_bf[h]] + qe[0].all, wr=PO2_trk)
                    mm(o_ap, vl, Am[1].ap[:, h, :], False, False, rd=[vgm[s]] + Am[1].all, wr=PO2_trk)
                    mm(o_ap, sn.ap[:, h, c * 128:(c + 1) * 128], qe[1].ap[:, h, :], False, True,
                       rd=sn.all + qe[1].all, wr=PO2_trk)
            for h in range(4):
                pU = nb()
                mm(pU.ap[:, 0:256], kdm.ap[:, h * 128:(h + 1) * 128], vgm.ap[:, s, h * 256:(h + 1) * 256], True, True,
                   rd=kdm.all + [vgm[s]], wr=pU.all)
                stt(Sf.ap[:, h, :], Sf.ap[:, h, :], Eq[0].ap[:, h, 127:128], pU.ap[:, 0:256], ALU.mult, ALU.add,
                    rd=[Sf[h]] + Eq[0].all + pU.all, wr=[Sf[h]])
                cpy("act", Sf_bf.ap[:, h, :], Sf.ap[:, h, :], rd=[Sf[h]], wr=[Sf_bf[h]])
            o8 = PO2[:, :].rearrange("p (a t) -> p a t", a=8)
            for b2 in range(2):
                act(osq.ap[:, 4 * b2:4 * b2 + 4, :], o8[:, 4 * b2:4 * b2 + 4, :], AF.Square,
                    rd=PO2_trk, wr=osq.all)
            pss = nb()
            for h in range(4):
                for c in range(2):
                    mm(pss.ap[:, h * 128:(h + 1) * 128], ones_bf.ap[:], osq.ap[:, h * 2 + c, :], c == 0, c == 1,
                       rd=ones_bf.all + osq.all, wr=pss.all)
            act(rso.ap[:], pss.ap[:].rearrange("p (h t) -> p h t", h=4), AF.Ln, rd=pss.all, wr=rso.all,
                scale=1.0 / 256, bias=EPS)
            act(rro.ap[:], rso.ap[:], AF.Exp, rd=rso.all, wr=rro.all, scale=-0.5)
            for b2 in range(2):
                tt("dve", t8.ap[:, 4 * b2:4 * b2 + 4, :], o8[:, 4 * b2:4 * b2 + 4, :],
                   sgn.ap[:, 4 * b2:4 * b2 + 4, s * 128:(s + 1) * 128], ALU.mult, rd=PO2_trk + sgn.all, wr=t8.all)
            tt("pool", glaO.ap[:, :, s * 128:(s + 1) * 128].rearrange("p (h c) t -> p h c t", c=2),
               t8.ap[:].rearrange("p (h c) t -> p h c t", c=2),
               rro.ap[:].unsqueeze(2).broadcast_to([128, 4, 2, 128]), ALU.mult,
               rd=t8.all + rro.all, wr=glaO.all)
        S.barrier()

        S.tag = "P3.merge"
        sb.reset(zm)
        sigA = sb.alloc([128, 4, 512], F32)
        sigB = sb.alloc([128, 4, 512], F32)
        part = sb.alloc([128, 4, 512], F32)
        mt1 = sb.alloc([128, 512], F32)
        sb.cur = zm + 32768
        mixed = sb.alloc([128, KC, 512], BF16, KC)
        for cg in range(4):
            for nm, sg_t, bcol in (("ga", sigA, VI_BMA), ("gb", sigB, VI_BMB)):
                w = wnext("%s%d" % (nm, cg))
                for c4 in range(4):
                    pg = nb()
                    for k in range(KC):
                        mm(pg.ap[:], w.ap[:, k, c4 * 128:(c4 + 1) * 128], xg.ap[:, k, :], k == 0, k == KC - 1,
                           rd=w.all + [xg[k]], wr=pg.all)
                    tt("dve", mt1.ap[:], pg.ap[:], rstd_b.ap[:], ALU.mult, rd=pg.all + rstd_b.all, wr=mt1.all)
                    act(sg_t.ap[:, c4, :], mt1.ap[:], AF.Sigmoid, rd=mt1.all + vec.all, wr=sg_t.all,
                        bias=V(bcol + cg * 4 + c4))
            w = wnext("ap%d" % cg)
            for c4 in range(4):
                pa = nb()
                for k in range(8):
                    mm(pa.ap[:], w.ap[:, k, c4 * 128:(c4 + 1) * 128], attnO.ap[:, k, :], k == 0, k == 7,
                       rd=w.all + [attnO[k]], wr=pa.all)
                tt("dve", part.ap[:, c4, :], pa.ap[:], sigA.ap[:, c4, :], ALU.mult, rd=pa.all + sigA.all,
                   wr=part.all)
            w = wnext("gp%d" % cg)
            for c4 in range(4):
                pa = nb()
                for k in range(8):
                    mm(pa.ap[:], w.ap[:, k, c4 * 128:(c4 + 1) * 128], glaO.ap[:, k, :], k == 0, k == 7,
                       rd=w.all + glaO.all, wr=pa.all)
                tt("dve", mt1.ap[:], pa.ap[:], sigB.ap[:, c4, :], ALU.mult, rd=pa.all + sigB.all, wr=mt1.all)
                tt("pool", mixed.ap[:, cg * 4 + c4, :], mt1.ap[:], part.ap[:, c4, :], ALU.add,
                   rd=mt1.all + part.all, wr=[mixed[cg * 4 + c4]])
        S.barrier()

        S.tag = "P3.wout"
        sb.reset(zm)
        hT = sb.alloc([128, KC, 512], F32, KC)
        sb.cur = zm + 49152
        rtmp = sb.alloc([128, 512], BF16)
        ytile = [sb.alloc([128, 512], F32) for _ in range(2)]
        ss2 = PB[6]
        for cg in range(4):
            w = wnext("wo%d" % cg)
            for c4 in range(4):
                c = cg * 4 + c4
                po = nb()
                for k in range(KC):
                    mm(po.ap[:], w.ap[:, k, c4 * 128:(c4 + 1) * 128], mixed.ap[:, k, :], k == 0, k == KC - 1,
                       rd=w.all + [mixed[k]], wr=po.all)
                if c == 0:
                    for c_ in range(2):
                        ld(xring[c_ % 3].ap[:], xov[:, c_, tok0:tok0 + 512], wr=xring[c_ % 3].all)
                if c + 2 < KC:
                    ld(xring[(c + 2) % 3].ap[:], xov[:, c + 2, tok0:tok0 + 512], wr=xring[(c + 2) % 3].all)
                xr = xring[c % 3]
                tt("dve", hT.ap[:, c, :], po.ap[:], xr.ap[:], ALU.add, rd=po.all + xr.all, wr=[hT[c]])
                if c >= 1:
                    sqp = sqring[(c - 1) % 2]
                    mm(ss2.ap[:], ones_bf.ap[:], sqp.ap[:], c == 1, False, rd=ones_bf.all + sqp.all, wr=ss2.all)
                sq = sqring[c % 2]
                act(sq.ap[:], hT.ap[:, c, :], AF.Square, rd=[hT[c]], wr=sq.all)
        sqp = sqring[(KC - 1) % 2]
        mm(ss2.ap[:], ones_bf.ap[:], sqp.ap[:], False, True, rd=ones_bf.all + sqp.all, wr=ss2.all)
        mk_rstd(ss2, 1.0 / D, rstd_b)
        for c in range(KC):
            stt(xg.ap[:, c, :], hT.ap[:, c, :], V(VI_GMLP + c), rstd_b.ap[:], ALU.mult, ALU.mult,
                rd=[hT[c]] + vec.all + rstd_b.all, wr=[xg[c]])
        S.barrier()

        S.tag = "P3.mlp"
        sb.cur = zm + 32768
        uG = sb.alloc([128, KC, 512], BF16, KC)
        for G in range(4):
            for j in range(4):
                w = wnext("up%d_%d" % (G, j))
                for c4 in range(4):
                    f = j * 4 + c4
                    pu = nb()
                    for k in range(KC):
                        mm(pu.ap[:], w.ap[:, k, c4 * 128:(c4 + 1) * 128], xg.ap[:, k, :], k == 0, k == KC - 1,
                           rd=w.all + [xg[k]], wr=pu.all)
                    act(rtmp.ap[:], pu.ap[:], AF.Relu, rd=pu.all, wr=rtmp.all)
                    tt("pool", uG.ap[:, f, :], rtmp.ap[:], rtmp.ap[:], ALU.mult, rd=rtmp.all, wr=[uG[f]])
            for cg in range(4):
                w = wnext("dn%d_%d" % (G, cg))
                for c4 in range(4):
                    c = cg * 4 + c4
                    pd = nb()
                    for f in range(KC):
                        mm(pd.ap[:], w.ap[:, f, c4 * 128:(c4 + 1) * 128], uG.ap[:, f, :], f == 0, f == KC - 1,
                           rd=w.all + [uG[f]], wr=pd.all)
                    tt("dve", hT.ap[:, c, :], hT.ap[:, c, :], pd.ap[:], ALU.add, rd=[hT[c]] + pd.all, wr=[hT[c]])
        S.tag = "P3.final"
        for c in range(KC):
            sq = sqring[c % 2]
            act(sq.ap[:], hT.ap[:, c, :], AF.Square, rd=[hT[c]], wr=sq.all)
            mm(ss2.ap[:], ones_bf.ap[:], sq.ap[:], c == 0, c == KC - 1, rd=ones_bf.all + sq.all, wr=ss2.all)
        mk_rstd(ss2, 1.0 / D, rstd_b)
        for c in range(KC):
            yt = ytile[c % 2]
            stt(yt.ap[:], hT.ap[:, c, :], V(VI_GFIN + c), rstd_b.ap[:], ALU.mult, ALU.mult,
                rd=[hT[c]] + vec.all + rstd_b.all, wr=yt.all)
            st(yT[c * 128:(c + 1) * 128, tok0:tok0 + 512], yt.ap[:], rd=yt.all, is_out=True)
        S.barrier()

    S.emit(sems, dsems)
    nc._tagnames = S.names
    print("SBUF peak", sb.peak, "of", sb.top, " instrs:", {n: len(S.q[n].ins) for n in Sched.ENGS})
    return nc


def prep_in_maps(inputs, cfg, cores=range(8)):
    f = np.float32
    TP, TS, OP, OS = cfg.TP, cfg.TS, cfg.OP, cfg.OS
    xpa = np.asarray(inputs["x_prompt"], f)
    xsT = np.ascontiguousarray(np.asarray(inputs["x_sample"], f)[0].T)
    consts = host_consts()
    cs = host_rope(TS)
    g = lambda k: np.asarray(inputs[k], f)
    z16 = np.zeros((16, 512), f)
    wgu = np.stack([np.concatenate([g("w_gate_up_fwd")[0], z16, g("b_gate_fwd")[0][None]], 0),
                    np.concatenate([z16, g("w_gate_up_bwd")[0], g("b_gate_bwd")[0][None]], 0)], 0)
    rows = np.concatenate([g("q_norm")[0], g("k_norm")[0]])[None, :]
    base = np.zeros((128, NV), f)
    base[:, VI_GMIX:VI_GMIX + 16] = g("norm_mix")[0].reshape(16, 128).T
    base[:, VI_GMLP:VI_GMLP + 16] = g("norm_mlp")[0].reshape(16, 128).T
    base[:, VI_GFIN:VI_GFIN + 16] = g("norm_final").reshape(16, 128).T
    base[:, VI_GQ] = g("q_norm")[0]
    base[:, VI_GK] = g("k_norm")[0]
    base[:, VI_GN:VI_GN + 2] = g("gla_norm")[0].reshape(2, 128).T
    base[:, VI_BMA:VI_BMA + 16] = g("b_merge")[0][:D].reshape(16, 128).T
    base[:, VI_BMB:VI_BMB + 16] = g("b_merge")[0][D:].reshape(16, 128).T
    base[:, VI_NFLAG:VI_NFLAG + 10] = 1.0
    base[:, VI_NEG16:VI_NEG16 + 2] = -1.0 / 16
    base[:, VI_ONE:VI_ONE + 2] = 1.0
    shared = {"consts": consts, "rows": np.ascontiguousarray(rows), "wgu": wgu,
              "w_in": g("w_in")[0], "w_ap": g("w_attn_proj")[0], "w_gp": g("w_gla_proj")[0],
              "w_out": g("w_out")[0], "w_up": g("w_up")[0], "w_down": g("w_down")[0]}
    maps = []
    xpT_cache = {}
    for c in cores:
        b, h = c // 2, c % 2
        if b not in xpT_cache:
            xpT_cache[b] = np.ascontiguousarray(xpa[b].T)
        xpT = xpT_cache[b]
        vecs = base.copy()
        vecs[:, VI_FLAG + h] = 1.0
        vecs[:, VI_NFLAG + h] = 0.0
        vecs[:, VI_FLAG + 2 + c] = 1.0
        vecs[:, VI_NFLAG + 2 + c] = 0.0
        m = dict(shared)
        po = [s_ for s_ in range(NSEG_P) if s_ != h] + [h]
        so = [s_ for s_ in range(NSEG_S) if s_ != c] + [c]
        m["xp"] = np.ascontiguousarray(np.concatenate([xpT[:, s_ * OP:(s_ + 1) * OP] for s_ in po], 1))
        m["xs"] = np.ascontiguousarray(np.concatenate([xsT[:, s_ * OS:(s_ + 1) * OS] for s_ in so], 1))
        m["cs_cp"] = np.ascontiguousarray(np.concatenate([cs[:, :, s_ * OP:(s_ + 1) * OP] for s_ in po], 2))
        m["cs_cs"] = np.ascontiguousarray(np.concatenate([cs[:, :, s_ * OS:(s_ + 1) * OS] for s_ in so], 2))
        m["xo"] = np.ascontiguousarray(np.concatenate([xpT[:, h * OP:(h + 1) * OP], xsT[:, c * OS:(c + 1) * OS]], 1))
        m["cs_o"] = np.ascontiguousarray(np.concatenate([cs[:, :, h * OP:(h + 1) * OP],
                                                          cs[:, :, c * OS:(c + 1) * OS]], 2))
        m["vecs"] = vecs
        maps.append(m)
    return maps


def assemble(results, cfg, cores=range(8)):
    TP, TS, OP, OS = cfg.TP, cfg.TS, cfg.OP, cfg.OS
    yp = np.zeros((4, TP, D), np.float32)
    ys = np.zeros((1, TS, D), np.float32)
    for r, c in zip(results, cores):
        b, h = c // 2, c % 2
        y = np.asarray(r["yT"])
        yp[b, h * OP:(h + 1) * OP, :] = y[:, :OP].T
        ys[0, c * OS:(c + 1) * OS, :] = y[:, OP:].T
    return yp, ys


def kernel(**inputs):
    cfg = Cfg()
    nc = build(cfg)
    maps = prep_in_maps(inputs, cfg)
    res = run_bass_kernel_spmd(nc, maps, core_ids=list(range(8)))
    return assemble(res.results, cfg)
```

```python
import numpy as np
import concourse.bass as bass
import concourse.mybir as mybir
from concourse.bass_utils import run_bass_kernel_spmd

F32 = mybir.dt.float32
BF16 = mybir.dt.bfloat16
AF = mybir.ActivationFunctionType
ALU = mybir.AluOpType


class Trk:
    __slots__ = ("w", "rc", "rd")

    def __init__(self):
        self.w = None
        self.rc = {}
        self.rd = []


class Ins:
    __slots__ = ("fn", "waits", "inc", "dsem", "dcount", "tag")

    def __init__(self, fn):
        self.fn = fn
        self.tag = ""
        self.waits = []
        self.inc = False
        self.dsem = None
        self.dcount = 0


class EngQ:
    def __init__(self, name):
        self.name = name
        self.ins = []
        self.seen_c = {}
        self.seen_d = set()
        self.sem = None


class Sched:
    ENGS = ("pe", "act", "dve", "pool", "sp")

    def __init__(self, nc, n_dsem=40):
        self.nc = nc
        self.q = {n: EngQ(n) for n in self.ENGS}
        self.n_dsem = n_dsem
        self.dsem_last = [None] * n_dsem
        self.dsem_cnt = [0] * n_dsem
        self.dsem_next = 0
        self.out_dmas = []
        self.tag = ""
        self.names = {}

    def _deps(self, eng, rd, wr, same_ok):
        dc = {}
        dd = []

        def add(dep):
            if dep is None:
                return
            if dep[0] == 'c':
                if dc.get(dep[1], -1) < dep[2]:
                    dc[dep[1]] = dep[2]
            else:
                dd.append(dep[1])

        for t in rd:
            add(t.w)
        for t in wr:
            add(t.w)
            for en, idx in t.rc.items():
                add(('c', en, idx))
            for d in t.rd:
                add(('d', d))
        waits = []
        for en, idx in dc.items():
            if en == eng.name and same_ok:
                continue
            if eng.seen_c.get(en, -1) >= idx:
                continue
            eng.seen_c[en] = idx
            self.q[en].ins[idx].inc = True
            waits.append(('c', en, idx))
        for d in dd:
            if id(d) in eng.seen_d:
                continue
            eng.seen_d.add(id(d))
            waits.append(('d', d))
        return waits

    def op(self, en, fn, rd=(), wr=(), same_ok=False):
        eng = self.q[en]
        ins = Ins(fn)
        ins.tag = self.tag
        ins.waits = self._deps(eng, rd, wr, same_ok)
        eng.ins.append(ins)
        me = ('c', en, len(eng.ins) - 1)
        for t in rd:
            if t.rc.get(en, -1) < me[2]:
                t.rc[en] = me[2]
        for t in wr:
            t.w = me
            t.rc = {}
            t.rd = []
        return ins

    def dma(self, en, fn, rd=(), wr=(), is_out=False):
        eng = self.q[en]
        ins = Ins(fn)
        ins.tag = self.tag
        ins.waits = self._deps(eng, rd, wr, False)
        k = self.dsem_next
        self.dsem_next = (k + 1) % self.n_dsem
        prev = self.dsem_last[k]
        if prev is not None and id(prev) not in eng.seen_d:
            eng.seen_d.add(id(prev))
            ins.waits.append(('d', prev))
        self.dsem_cnt[k] += 16
        ins.dsem = k
        ins.dcount = self.dsem_cnt[k]
        self.dsem_last[k] = ins
        eng.ins.append(ins)
        me = ('d', ins)
        for t in rd:
            t.rd.append(ins)
        for t in wr:
            t.w = me
            t.rc = {}
            t.rd = []
        if is_out:
            self.out_dmas.append(ins)
        return ins

    def barrier(self):
        last = {n: len(self.q[n].ins) - 1 for n in self.ENGS}
        dmas = [d for d in self.dsem_last if d is not None]
        for n in self.ENGS:
            eng = self.q[n]
            ins = Ins(None)
            for m in self.ENGS:
                if m == n or last[m] < 0:
                    continue
                idx = last[m]
                while idx >= 0 and self.q[m].ins[idx].dsem is not None:
                    idx -= 1
                while idx >= 0 and self.q[m].ins[idx].fn is None:
                    idx -= 1
                if idx < 0 or eng.seen_c.get(m, -1) >= idx:
                    continue
                eng.seen_c[m] = idx
                self.q[m].ins[idx].inc = True
                ins.waits.append(('c', m, idx))
            for d in dmas:
                if id(d) in eng.seen_d:
                    continue
                eng.seen_d.add(id(d))
                ins.waits.append(('d', d))
            eng.ins.append(ins)

    def emit(self, sems, dsems):
        nc = self.nc
        for n in self.ENGS:
            self.q[n].sem = sems[n]
        counts = {}
        for n in self.ENGS:
            c = 0
            arr = []
            for ins in self.q[n].ins:
                if ins.inc and ins.dsem is None and ins.fn is not None:
                    c += 1
                arr.append(c)
            counts[n] = arr

        def run(n, e):
            eng = self.q[n]
            for ins in eng.ins:
                for w in ins.waits:
                    if w[0] == 'c':
                        e.wait_ge(sems[w[1]], counts[w[1]][w[2]])
                    else:
                        e.wait_ge(dsems[w[1].dsem], w[1].dcount)
                if ins.fn is None:
                    continue
                r = ins.fn(e)
                try:
                    self.names[r.ins.name] = ins.tag
                except Exception:
                    pass
                if ins.dsem is not None:
                    r.then_inc(dsems[ins.dsem], 16)
                elif ins.inc:
                    r.then_inc(sems[n], 1)
            if n == "sp":
                for d in self.out_dmas:
                    e.wait_ge(dsems[d.dsem], d.dcount)

        with nc.Block() as block:
            @block.tensor
            def _(e):
                run("pe", e)

            @block.scalar
            def _(e):
                run("act", e)

            @block.vector
            def _(e):
                run("dve", e)

            @block.gpsimd
            def _(e):
                run("pool", e)

            @block.sync
            def _(e):
                run("sp", e)


class T:
    def __init__(self, ap, ncell=1):
        self.ap = ap
        self.c = [Trk() for _ in range(ncell)]

    def __getitem__(self, i):
        return self.c[i]

    @property
    def all(self):
        return self.c


D = 2048
KC = 16
DIN = 8736
DFF = 8192
C_QA, C_KA, C_VA, C_QG, C_KG, C_VG, C_GG, C_LRF, C_LRB, C_GA, C_GB = (
    0, 1024, 1280, 1536, 2048, 2560, 3584, 4608, 4624, 4640, 6688)
EPS = 1e-6
NSEG_P, NSEG_S = 2, 8
MT = 512
OPT_B1 = False
OPT_B2 = True
OPT_C = False
SUB = MT // 128


class Cfg:
    def __init__(self, TP=4096, TS=16384):
        self.TP, self.TS = TP, TS
        self.OP, self.OS = TP // NSEG_P, TS // NSEG_S
        self.OWN = self.OP + self.OS


CI_PERM, CI_LF, CI_LB, CI_UF, CI_UB, CI_MF, CI_MB, CI_ONES = [i * 128 for i in range(8)]
NCONST = 8 * 128
VI_GMIX, VI_GMLP, VI_GFIN = 0, 16, 32
VI_GQ, VI_GK = 48, 49
VI_GN = 50
VI_BMA, VI_BMB = 52, 68
VI_FLAG = 84
VI_NFLAG = 94
VI_NEG16 = 104
VI_ONE = 106
NV = 108


def host_consts():
    c = np.zeros((128, NCONST), np.float32)
    j = np.arange(128)[:, None]
    i = np.arange(128)[None, :]
    pm = np.zeros((128, 128), np.float32)
    for m in range(128):
        sec = (m // 64) * 64
        r = m - sec
        if r < 32:
            pm[m, sec + r + 32] = -1.0
        else:
            pm[m, sec + r - 32] = 1.0
    c[:, CI_PERM:CI_PERM + 128] = pm.T
    c[:, CI_LF:CI_LF + 128] = (j <= i) * (-1.0 / 16)
    c[:, CI_LB:CI_LB + 128] = (j >= i) * (-1.0 / 16)
    c[:, CI_UF:CI_UF + 128] = (j > i) * (-1.0 / 16)
    c[:, CI_UB:CI_UB + 128] = (j < i) * (-1.0 / 16)
    c[:, CI_MF:CI_MF + 128] = (j <= i) * 1.0
    c[:, CI_MB:CI_MB + 128] = (j > i) * 1.0
    c[:, CI_ONES:CI_ONES + 128] = 1.0
    return c


def host_rope(T):
    t = np.arange(T, dtype=np.float32)
    rows = np.floor(t / 64).astype(np.float32)
    cols = (t - rows * 64).astype(np.float32)
    sec = 64
    inv = (np.float32(10000.0) ** (-np.arange(0, sec, 2, dtype=np.float32) / np.float32(sec))).astype(np.float32)
    ang_r = (rows[None, :] * inv[:, None]).astype(np.float32)
    ang_c = (cols[None, :] * inv[:, None]).astype(np.float32)
    cs = np.zeros((128, 2, T), np.float32)
    cs[0:32, 0] = np.cos(ang_r); cs[32:64, 0] = np.cos(ang_r)
    cs[64:96, 0] = np.cos(ang_c); cs[96:128, 0] = np.cos(ang_c)
    cs[0:32, 1] = np.sin(ang_r); cs[32:64, 1] = np.sin(ang_r)
    cs[64:96, 1] = np.sin(ang_c); cs[96:128, 1] = np.sin(ang_c)
    return cs


class SBAlloc:
    def __init__(self, nc, base, top):
        self.nc, self.cur, self.top, self.n = nc, base, top, 0
        self.peak = base

    def alloc(self, shape, dt, ncell=1):
        per = 1
        for s in shape[1:]:
            per *= s
        nbytes = per * (2 if dt == BF16 else 4)
        off = (self.cur + 63) // 64 * 64
        self.cur = off + nbytes
        self.peak = max(self.peak, self.cur)
        assert self.cur <= self.top, ("SBUF overflow", self.cur, self.top)
        h = self.nc.alloc_sbuf_tensor_at("sb%d" % self.n, list(shape), dt, offset=off)
        self.n += 1
        return T(h, ncell)

    def mark(self):
        return self.cur

    def reset(self, m):
        self.cur = m


def build(cfg, debug=False):
    nc = bass.Bass("TRN2", target_bir_lowering=False)
    S = Sched(nc, n_dsem=64)
    TP, TS, OP, OS, OWN = cfg.TP, cfg.TS, cfg.OP, cfg.OS, cfg.OWN
    NOWN_SUB = OWN // 128

    def din(name, shape, dt=F32):
        return nc.dram_tensor(name, list(shape), dt, kind="ExternalInput").ap()

    def dscr(name, shape, dt=BF16):
        return nc.dram_tensor(name, list(shape), dt, kind="Internal").ap()

    def dout(name, shape, dt=F32):
        return nc.dram_tensor(name, list(shape), dt, kind="ExternalOutput").ap()

    xo = din("xo", [D, OWN]); xp = din("xp", [D, TP]); xs = din("xs", [D, TS])
    cs_o = din("cs_o", [128, 2, OWN]); cs_cp = din("cs_cp", [128, 2, TP]); cs_cs = din("cs_cs", [128, 2, TS])
    consts_d = din("consts", [128, NCONST]); vecs_d = din("vecs", [128, NV])
    rows_d = din("rows", [1, 256]); wgu_d = din("wgu", [2, 33, 512])
    w_in = din("w_in", [D, DIN]); w_ap = din("w_ap", [1024, D]); w_gp = din("w_gp", [1024, D])
    w_out = din("w_out", [D, D]); w_up = din("w_up", [D, DFF]); w_down = din("w_down", [DFF, D])
    yT = dout("yT", [D, OWN])

    wb_in = dscr("wb_in", [D, DIN]); wb_ap = dscr("wb_ap", [1024, D]); wb_gp = dscr("wb_gp", [1024, D])
    wb_out = dscr("wb_out", [D, D]); wb_up = dscr("wb_up", [D, DFF]); wb_down = dscr("wb_down", [DFF, D])
    Kt = {"P": dscr("KtP", [2, 128, TP]), "S": dscr("KtS", [2, 128, TS])}
    Vs = {"P": dscr("VP", [TP, 256]), "S": dscr("VS", [TS, 256])}
    SbScr = dscr("SbScr", [NOWN_SUB, 128, 4, 256])
    dbg_outs = {}

    sems = {n: nc.alloc_semaphore("s_" + n) for n in Sched.ENGS}
    dsems = [nc.alloc_semaphore("d%d" % i) for i in range(64)]

    sb = SBAlloc(nc, 16512, 229344 - 2048)
    pbig = [nc.alloc_psum_tensor("ps%d" % i, [128, 1024], F32) for i in range(4)]
    PB = [T(pbig[i // 2][:, (i % 2) * 512:(i % 2 + 1) * 512]) for i in range(8)]
    rot = [0]

    def nb(allowed=range(6)):
        allowed = list(allowed)
        b = allowed[rot[0] % len(allowed)]
        rot[0] += 1
        return PB[b]

    def mm(out, lhsT, rhs, start, stop, rd, wr):
        S.op("pe", lambda e: e.matmul(out, lhsT=lhsT, rhs=rhs, start=start, stop=stop),
             rd=rd, wr=wr, same_ok=True)

    def act(out, in_, func, rd, wr, bias=None, scale=None):
        kw = {}
        if bias is not None:
            kw["bias"] = bias
        if scale is not None:
            kw["scale"] = scale
        S.op("act", lambda e: e.activation(out=out, in_=in_, func=func, **kw), rd=rd, wr=wr)

    def tt(en, out, a, b, op, rd, wr):
        S.op(en, lambda e: e.tensor_tensor(out=out, in0=a, in1=b, op=op), rd=rd, wr=wr)

    def ts(en, out, a, s1, op0, rd, wr, s2=None, op1=None):
        if op1 is None:
            S.op(en, lambda e: e.tensor_scalar(out=out, in0=a, scalar1=s1, scalar2=None, op0=op0), rd=rd, wr=wr)
        else:
            S.op(en, lambda e: e.tensor_scalar(out=out, in0=a, scalar1=s1, scalar2=s2, op0=op0, op1=op1),
                 rd=rd, wr=wr)

    def stt(out, a, sc, b, op0, op1, rd, wr):
        S.op("dve", lambda e: e.scalar_tensor_tensor(out=out, in0=a, scalar=sc, in1=b, op0=op0, op1=op1),
             rd=rd, wr=wr)

    def recip(out, in_, rd, wr):
        S.op("dve", lambda e: e.reciprocal(out=out, in_=in_), rd=rd, wr=wr)

    def cpy(en, out, in_, rd, wr):
        if en == "act":
            act(out, in_, AF.Copy, rd, wr)
        else:
            S.op(en, lambda e: e.tensor_copy(out=out, in_=in_), rd=rd, wr=wr)

    def ld(out, in_, rd=(), wr=()):
        S.dma("sp", lambda e: e.dma_start(out=out, in_=in_), rd=rd, wr=wr)

    def st(out, in_, rd=(), wr=(), is_out=False):
        S.dma("pool", lambda e: e.dma_start(out=out, in_=in_), rd=rd, wr=wr, is_out=is_out)

    def memset(en, ap, val, wr):
        S.op(en, lambda e: e.memset(ap, val), rd=(), wr=wr)

    cst = sb.alloc([128, NCONST], F32)
    vec = sb.alloc([128, NV], F32)
    wgu = sb.alloc([33, 2, 512], BF16)
    ones_bf = sb.alloc([128, 128], BF16)
    negb = sb.alloc([128, 2], F32)
    lrT = sb.alloc([64, 512], BF16)
    Sf = sb.alloc([128, 4, 256], F32, 4)
    Sfin = {"P": sb.alloc([128, 4, 256], F32, 4), "S": sb.alloc([128, 4, 256], F32, 4)}
    rstd_b = sb.alloc([128, 512], F32)
    rcol = sb.alloc([128, 8], F32)
    xg = sb.alloc([128, KC, 512], BF16, KC)
    xring = [sb.alloc([128, 512], F32) for _ in range(3)]
    sqring = [sb.alloc([128, 512], BF16) for _ in range(2)]
    tmpA = sb.alloc([128, 512], F32)
    ssacc = [sb.alloc([128, 512], F32) for _ in range(2)]
    nr_sets = [[sb.alloc([128, 512], F32) for _ in range(6)] + [sb.alloc([128, 512], BF16)]]
    gsc = sb.alloc([128, 2], F32)
    permG = sb.alloc([128, 2, 128], F32)
    msetup = sb.mark()
    rows = sb.alloc([1, 256], F32)
    wgu32 = sb.alloc([33, 2, 512], F32)

    ld(cst.ap[:], consts_d, wr=cst.all)
    ld(vec.ap[:], vecs_d, wr=vec.all)
    ld(rows.ap[:], rows_d, wr=rows.all)
    ld(wgu32.ap[:], wgu_d.rearrange("a r c -> r a c"), wr=wgu32.all)
    cpy("dve", wgu.ap[:], wgu32.ap[:], rd=wgu32.all, wr=wgu.all)
    cpy("dve", ones_bf.ap[:], cst.ap[:, CI_ONES:CI_ONES + 128], rd=cst.all, wr=ones_bf.all)
    memset("pool", lrT.ap[:], 1.0, wr=lrT.all)

    def C(i, n=128):
        return cst.ap[:, i:i + n]

    ts("dve", gsc.ap[:, 0:1], vec.ap[:, VI_GQ:VI_GQ + 1], 128.0 ** -0.5, ALU.mult, rd=vec.all, wr=gsc.all)
    cpy("dve", gsc.ap[:, 1:2], vec.ap[:, VI_GK:VI_GK + 1], rd=vec.all + gsc.all, wr=gsc.all)
    for i_ in range(2):
        ts("dve", permG.ap[:, i_, :], cst.ap[:, CI_PERM:CI_PERM + 128], gsc.ap[:, i_:i_ + 1], ALU.mult,
           rd=cst.all + gsc.all, wr=permG.all)

    def V(i, n=1):
        return vec.ap[:, i:i + n]

    mx = sb.alloc([1, 4], F32)
    S.op("dve", lambda e: e.tensor_reduce(out=mx.ap[:, 0:1], in_=rows.ap[:, 0:128], axis=mybir.AxisListType.X,
                                          op=ALU.max, apply_absolute_value=True), rd=rows.all, wr=mx.all)
    S.op("dve", lambda e: e.tensor_reduce(out=mx.ap[:, 1:2], in_=rows.ap[:, 128:256], axis=mybir.AxisListType.X,
                                          op=ALU.max, apply_absolute_value=True), rd=rows.all + mx.all, wr=mx.all)
    tt("dve", mx.ap[:, 2:3], mx.ap[:, 0:1], mx.ap[:, 1:2], ALU.mult, rd=mx.all, wr=mx.all)
    ts("dve", mx.ap[:, 2:4], mx.ap[:, 2:3].broadcast_to([1, 2]), -(128.0 ** 0.5), ALU.mult, rd=mx.all, wr=mx.all)
    pbk = nb()
    mm(pbk.ap[:, 0:2], cst.ap[0:1, CI_ONES:CI_ONES + 128], mx.ap[0:1, 2:4], True, True,
       rd=cst.all + mx.all, wr=pbk.all)
    cpy("dve", negb.ap[:], pbk.ap[:, 0:2], rd=pbk.all, wr=negb.all)
    S.barrier()
    sb.reset(msetup)

    S.tag = "P0"
    cin = [sb.alloc([128, 1024], F32) for _ in range(2)]
    cout = [sb.alloc([128, 1024], BF16) for _ in range(2)]
    ci = [0]

    cring = {"n": 2}

    def cast_block(src, dst, r0, r1, c0, c1):
        for r in range(r0, r1):
            for cc in range(c0, c1, 1024):
                cw = min(1024, c1 - cc)
                i = ci[0]
                ci[0] += 1
                a, b = cin[i % cring["n"]], cout[i % cring["n"]]
                tg = S.tag
                S.tag = "P0"
                ld(a.ap[:, :cw], src[r * 128:(r + 1) * 128, cc:cc + cw], wr=a.all)
                cpy((("act", "act", "pool") if OPT_C else ("act", "dve", "pool"))[i % 3], b.ap[:, :cw],
                    a.ap[:, :cw], rd=a.all, wr=b.all)
                st(dst[r * 128:(r + 1) * 128, cc:cc + cw], b.ap[:, :cw], rd=b.all)
                S.tag = tg
                yield

    def cast_rest():
        yield from cast_block(w_in, wb_in, 0, KC, 0, C_KA)
        yield from cast_block(w_in, wb_in, 0, KC, C_KA + 512, C_KG)
        yield from cast_block(w_in, wb_in, 0, KC, C_GG, C_LRF)
        yield from cast_block(w_in, wb_in, 0, KC, C_LRF + 32, DIN)
        yield from cast_block(w_ap, wb_ap, 0, 8, 0, D)
        yield from cast_block(w_gp, wb_gp, 0, 8, 0, D)
        yield from cast_block(w_out, wb_out, 0, KC, 0, D)
        yield from cast_block(w_up, wb_up, 0, KC, 0, DFF)
        yield from cast_block(w_down, wb_down, 0, 64, 0, D)

    mtmp = sb.mark()
    cin += [sb.alloc([128, 1024], F32) for _ in range(6)]
    cout += [sb.alloc([128, 1024], BF16) for _ in range(6)]
    cring["n"] = 8
    for _ in cast_block(w_in, wb_in, 0, KC, C_KA, C_KA + 512):
        pass
    for _ in cast_block(w_in, wb_in, 0, KC, C_KG, C_GG):
        pass
    for _ in cast_block(w_in, wb_in, 0, KC, C_LRF, C_LRF + 32):
        pass
    S.barrier()
    cring["n"] = 2
    sb.reset(mtmp)
    castg = cast_rest()
    n_cast_rest = (KC * (1 + 1 + 1 + 5)) + 16 + 16 + 2 * KC + KC * 8 + 128

    def pump_cast(n):
        for _ in range(n):
            next(castg, None)

    wv_in = wb_in.rearrange("(k p) c -> p k c", p=128)
    cur = {"xg": xg}

    def x_part1(src, t0, gcol, xg_t, sa, ring=None, pe_acc=None):
        sv = src.rearrange("(k p) t -> p k t", p=128)
        ring = ring or xring
        if len(ring) >= KC:
            for k in range(KC):
                ld(ring[k].ap[:], sv[:, k, t0:t0 + 512], wr=ring[k].all)
        for k in range(KC):
            xr = ring[k % len(ring)]
            sq = sqring[k % 2]
            if len(ring) < KC:
                ld(xr.ap[:], sv[:, k, t0:t0 + 512], wr=xr.all)
            act(sq.ap[:], xr.ap[:], AF.Square, rd=xr.all, wr=sq.all)
            ts("dve", xg_t.ap[:, k, :], xr.ap[:], V(gcol + k), ALU.mult, rd=xr.all + vec.all, wr=[xg_t[k]])
            if pe_acc is not None:
                if k >= 1:
                    sqp = sqring[(k - 1) % 2]
                    mm(pe_acc.ap[:], ones_bf.ap[:], sqp.ap[:], k == 1, False, rd=sqp.all + ones_bf.all,
                       wr=pe_acc.all)
            elif k == 0:
                cpy("dve" if OPT_B1 else "pool", sa.ap[:], sq.ap[:], rd=sq.all, wr=sa.all)
            else:
                tt("dve" if OPT_B1 else "pool", sa.ap[:], sa.ap[:], sq.ap[:], ALU.add, rd=sa.all + sq.all,
                   wr=sa.all)
            yield

    def x_part2(sa, ssb=None):
        if ssb is not None:
            sqp = sqring[(KC - 1) % 2]
            mm(ssb.ap[:], ones_bf.ap[:], sqp.ap[:], False, True, rd=sqp.all + ones_bf.all, wr=ssb.all)
        if ssb is None:
            ssb = nb()
            mm(ssb.ap[:], C(CI_ONES), sa.ap[:], True, True, rd=cst.all + sa.all, wr=ssb.all)
        mk_rstd(ssb, 1.0 / D, rstd_b)
        pc = nb()
        for s in range(SUB):
            mm(pc.ap[:, 2 * s:2 * s + 2], rstd_b.ap[0:1, s * 128:(s + 1) * 128], vec.ap[0:1, VI_ONE:VI_ONE + 2],
               True, True, rd=rstd_b.all + vec.all, wr=pc.all)
        cpy("dve", rcol.ap[:], pc.ap[:, 0:8], rd=pc.all, wr=rcol.all)

    def x_tile(src, t0, gcol, ring=None):
        ssb = PB[6] if OPT_B2 else None
        for _ in x_part1(src, t0, gcol, cur["xg"], ssacc[0], ring, pe_acc=ssb):
            pass
        x_part2(ssacc[0], ssb)

    def mk_rstd(ssb, inv_n, out_t, width=512):
        act(tmpA.ap[:, :width], ssb.ap[:, :width], AF.Ln, rd=ssb.all, wr=tmpA.all, scale=inv_n, bias=EPS)
        act(out_t.ap[:, :width], tmpA.ap[:, :width], AF.Exp, rd=tmpA.all, wr=out_t.all, scale=-0.5)


    def qk_nr1(ps, which, si):
        ty, ty2, trs, trr, t1, t2, tsq = nr_sets[si]
        tt("dve", ty.ap[:], ps.ap[:], rstd_b.ap[:], ALU.mult, rd=ps.all + rstd_b.all, wr=ty.all)
        act(tsq.ap[:], ty.ap[:], AF.Square, rd=ty.all, wr=tsq.all)

    def qk_nr2(which, si, cs_t, out_ap, out_trk):
        ty, ty2, trs, trr, t1, t2, tsq = nr_sets[si]
        pa, pb = nb(), nb()
        mm(pa.ap[:], ones_bf.ap[:], tsq.ap[:], True, True, rd=tsq.all + ones_bf.all, wr=pa.all)
        mm(pb.ap[:], permG.ap[:, which, :], ty.ap[:], True, True, rd=ty.all + permG.all, wr=pb.all)
        act(trs.ap[:], pa.ap[:], AF.Ln, rd=pa.all, wr=trs.all, scale=1.0 / 128, bias=EPS)
        act(trr.ap[:], trs.ap[:], AF.Exp, rd=trs.all, wr=trr.all, scale=-0.5)
        stt(t1.ap[:], ty.ap[:], gsc.ap[:, which:which + 1], cs_t.ap[:, 0, :], ALU.mult, ALU.mult,
            rd=ty.all + gsc.all + cs_t.all, wr=t1.all)
        tt("dve", t2.ap[:], pb.ap[:], cs_t.ap[:, 1, :], ALU.mult, rd=pb.all + cs_t.all, wr=t2.all)
        tt("pool", t1.ap[:], t1.ap[:], t2.ap[:], ALU.add, rd=t1.all + t2.all, wr=t1.all)
        tt("dve", out_ap, t1.ap[:], trr.ap[:], ALU.mult, rd=t1.all + trr.all, wr=out_trk)

    def lr_proj(wlr):
        xg_ = cur["xg"]
        pl = nb()
        for k in range(KC):
            mm(pl.ap[0:32, :], wlr.ap[:, k, 0:32], xg_.ap[:, k, :], k == 0, k == KC - 1,
               rd=wlr.all + [xg_[k]], wr=pl.all)
        tt("dve", lrT.ap[0:32, :], pl.ap[0:32, :], rstd_b.ap[0:32, :], ALU.mult,
           rd=pl.all + rstd_b.all, wr=lrT.all)

    def softplus_neg(d_, s, sp_t, e_t):
        pz = nb()
        mm(pz.ap[:], lrT.ap[0:33, s * 128:(s + 1) * 128], wgu.ap[0:33, d_, :], True, True,
           rd=lrT.all + wgu.all, wr=pz.all)
        act(e_t.ap[:], pz.ap[:], AF.Exp, rd=pz.all, wr=e_t.all, scale=-1.0)
        act(sp_t.ap[:], e_t.ap[:], AF.Ln, rd=e_t.all, wr=sp_t.all, bias=1.0)

    def tok_proj(w_t, s, cols, nb_allowed=range(6)):
        xg_ = cur["xg"]
        pk = nb(nb_allowed)
        n = cols.stop - cols.start
        for k in range(KC):
            mm(pk.ap[:, :n], xg_.ap[:, k, s * 128:(s + 1) * 128], w_t.ap[:, k, cols], k == 0, k == KC - 1,
               rd=[xg_[k]] + w_t.all, wr=pk.all)
        return pk

    S.tag = "P1"
    m1 = sb.mark()
    wkv = sb.alloc([128, KC, 512], BF16)
    wgk = sb.alloc([128, KC, 512], BF16)
    wgv = [sb.alloc([128, KC, 512], BF16) for _ in range(2)]
    wlr = sb.alloc([128, KC, 32], BF16)
    ld(wkv.ap[:], wv_in[:, :, C_KA:C_KA + 512], wr=wkv.all)
    ld(wgk.ap[:], wv_in[:, :, C_KG:C_KG + 512], wr=wgk.all)
    for i in range(2):
        ld(wgv[i].ap[:], wv_in[:, :, C_VG + 512 * i:C_VG + 512 * (i + 1)], wr=wgv[i].all)
    ld(wlr.ap[:], wv_in[:, :, C_LRF:C_LRF + 32], wr=wlr.all)
    xgB = sb.alloc([128, KC, 512], BF16, KC)
    xgs = [xg, xgB]
    Sbin = {"P": sb.alloc([128, 4, 256], F32, 4), "S": sb.alloc([128, 4, 256], F32, 4)}
    cs_t = sb.alloc([128, 2, 512], F32)
    kout = [sb.alloc([128, 512], BF16) for _ in range(2)]
    va = sb.alloc([128, SUB, 256], BF16)
    e_t = sb.alloc([128, 512], F32)
    sp = [[sb.alloc([128, 512], F32) for _ in range(2)] for _ in range(2)]
    ekd = [sb.alloc([128, 512], F32) for _ in range(2)]
    kg32 = [sb.alloc([128, 512], F32) for _ in range(2)]
    kd = [sb.alloc([128, 512], BF16) for _ in range(2)]
    vg = [sb.alloc([128, 1024], BF16) for _ in range(2)]
    dec = sb.alloc([128, 8], F32)
    Bcol = sb.alloc([128, 8], F32)
    eB = sb.alloc([128, 8], F32)
    totb = sb.alloc([128, 8], F32)
    sbf = sb.alloc([128, 4, 256], BF16, 4)

    def gla_kv(s, b):
        pk = tok_proj(wgk, s, slice(0, 512))
        act(kg32[b].ap[:], pk.ap[:], AF.Copy, rd=pk.all + rcol.all, wr=kg32[b].all,
            scale=rcol.ap[:, 2 * s:2 * s + 1])
        for i in range(2):
            pv = tok_proj(wgv[i], s, slice(0, 512))
            ts("dve", vg[b].ap[:, 512 * i:512 * (i + 1)], pv.ap[:], rcol.ap[:, 2 * s:2 * s + 1], ALU.mult,
               rd=pv.all + rcol.all, wr=vg[b].all)

    def kd_make(d_, b, umat_col, with_brow):
        pu = nb()
        mm(pu.ap[:], C(umat_col), sp[b][d_].ap[:], True, True, rd=cst.all + sp[b][d_].all, wr=pu.all)
        act(ekd[d_].ap[:], pu.ap[:], AF.Exp, rd=pu.all, wr=ekd[d_].all)
        tt("dve" if d_ == 0 else "pool", kd[d_].ap[:], kg32[b].ap[:], ekd[d_].ap[:], ALU.mult,
           rd=kg32[b].all + ekd[d_].all, wr=kd[d_].all)

    def decay_make(d_, b):
        pd = nb()
        for h in range(4):
            mm(pd.ap[:, 2 * h:2 * h + 2], sp[b][d_].ap[:, h * 128:(h + 1) * 128], V(VI_NEG16, 2), True, True,
               rd=sp[b][d_].all + vec.all, wr=pd.all)
        act(dec.ap[:], pd.ap[:, 0:8], AF.Exp, rd=pd.all, wr=dec.all)

    def state_U(d_, h, b):
        pU = nb()
        mm(pU.ap[:, 0:256], kd[d_].ap[:, h * 128:(h + 1) * 128], vg[b].ap[:, h * 256:(h + 1) * 256], True, True,
           rd=kd[d_].all + vg[b].all, wr=pU.all)
        return pU

    tot_tiles = (TP + TS) // 512
    cast_per_tile = -(-n_cast_rest // tot_tiles)
    for seq, src, T_, nseg, f0, cs_c in (("P", xp, TP, NSEG_P, 0, cs_cp), ("S", xs, TS, NSEG_S, NSEG_P, cs_cs)):
        seg_sub = (T_ // nseg) // 128
        nt = T_ // 512
        nt_gla = nt - (T_ // nseg) // 512
        Sacc, accf = Sbin[seq], Sfin[seq]
        for h in range(4):
            memset("pool", Sf.ap[:, h, :], 0.0, wr=[Sf[h]])
            memset("pool", Sacc.ap[:, h, :], 0.0, wr=[Sacc[h]])
            memset("pool", accf.ap[:, h, :], 0.0, wr=[accf[h]])
        memset("pool", Bcol.ap[:], 0.0, wr=Bcol.all)
        for _ in x_part1(src, 0, VI_GMIX, xgs[0], ssacc[0], pe_acc=PB[7]):
            pass
        for t in range(nt):
            t0 = t * 512
            cur["xg"] = xgs[t % 2]
            xg_ = cur["xg"]
            if t == 0:
                S.tag = 'P1.x2'
                x_part2(ssacc[t % 2], PB[7])
            S.tag = 'P1.kv'
            nxt = (x_part1(src, t0 + 512, VI_GMIX, xgs[(t + 1) % 2], ssacc[(t + 1) % 2], pe_acc=PB[7])
                   if t + 1 < nt else iter(()))

            def pump(n):
                for _ in range(n):
                    next(nxt, None)

            cq = [cast_per_tile]

            def tick():
                tg = S.tag
                S.tag = 'P1.x'
                next(nxt, None)
                S.tag = tg
                if cq[0] > 0:
                    cq[0] -= 1
                    pump_cast(1)
            ld(cs_t.ap[:], cs_c[:, :, t0:t0 + 512], wr=cs_t.all)
            def ka_proj(g):
                pk = nb()
                for k in range(KC):
                    mm(pk.ap[:], wkv.ap[:, k, g * 128:(g + 1) * 128], xg_.ap[:, k, :], k == 0, k == KC - 1,
                       rd=wkv.all + [xg_[k]], wr=pk.all)
                qk_nr1(pk, 1, 0)

            def va_proj(s):
                pv = tok_proj(wkv, s, slice(256, 512))
                act(va.ap[:, s, :], pv.ap[:, 0:256], AF.Copy, rd=pv.all + rcol.all, wr=va.all,
                    scale=rcol.ap[:, 2 * s:2 * s + 1])
            def kv_section():
                S.tag = 'P1.kv'
                ka_proj(0)
                tick()
                va_proj(0)
                tick()
                va_proj(1)
                tick()
                qk_nr2(1, 0, cs_t, kout[0].ap[:], kout[0].all)
                st(Kt[seq][0, :, t0:t0 + 512], kout[0].ap[:], rd=kout[0].all)
                tick()
                ka_proj(1)
                tick()
                va_proj(2)
                tick()
                va_proj(3)
                tick()
                qk_nr2(1, 0, cs_t, kout[1].ap[:], kout[1].all)
                st(Kt[seq][1, :, t0:t0 + 512], kout[1].ap[:], rd=kout[1].all)
                st(Vs[seq][t0:t0 + 512, :].rearrange("(s p) c -> p s c", p=128), va.ap[:], rd=va.all)
                tick()
            if t >= nt_gla:
                kv_section()
                pump(KC)
                if t + 1 < nt:
                    S.tag = 'P1.x2'
                    x_part2(ssacc[(t + 1) % 2], PB[7])
                pump_cast(cq[0])
                continue
            S.tag = 'P1.lr'
            lr_proj(wlr)
            tick()

            def stageA(s):
                b = s % 2
                S.tag = 'P1.A'
                gla_kv(s, b)
                softplus_neg(0, s, sp[b][0], e_t)
                softplus_neg(1, s, sp[b][1], e_t)

            def stageB1(s):
                b = s % 2
                S.tag = 'P1.B1'
                kd_make(0, b, CI_UF, False)
                kd_make(1, b, CI_UB, False)
                decay_make(0, b)
                act(eB.ap[:], Bcol.ap[:], AF.Exp, rd=Bcol.all, wr=eB.all)
                pdb = nb()
                for h in range(4):
                    mm(pdb.ap[:, 2 * h:2 * h + 2], sp[b][1].ap[:, h * 128:(h + 1) * 128], V(VI_NEG16, 2), True, True,
                       rd=sp[b][1].all + vec.all, wr=pdb.all)
                cpy("dve", totb.ap[:], pdb.ap[:, 0:8], rd=pdb.all, wr=totb.all)

            def stageB2(s):
                b = s % 2
                S.tag = 'P1.B2'
                gs = t * SUB + s
                seg = gs // seg_sub
                if gs % seg_sub == 0:
                    for h in range(4):
                        stt(accf.ap[:, h, :], Sf.ap[:, h, :], V(VI_FLAG + f0 + seg), accf.ap[:, h, :], ALU.mult,
                            ALU.add, rd=[Sf[h], accf[h]] + vec.all, wr=[accf[h]])
                for h in range(4):
                    pU = state_U(0, h, b)
                    stt(Sf.ap[:, h, :], Sf.ap[:, h, :], dec.ap[:, 2 * h:2 * h + 1], pU.ap[:, 0:256], ALU.mult,
                        ALU.add, rd=[Sf[h]] + dec.all + pU.all, wr=[Sf[h]])
                    pU = state_U(1, h, b)
                    stt(Sacc.ap[:, h, :], pU.ap[:, 0:256], eB.ap[:, 2 * h:2 * h + 1], Sacc.ap[:, h, :], ALU.mult,
                        ALU.add, rd=[Sacc[h]] + pU.all + eB.all, wr=[Sacc[h]])
                tt("dve", Bcol.ap[:], Bcol.ap[:], totb.ap[:], ALU.add, rd=Bcol.all + totb.all, wr=Bcol.all)
                if (gs + 1) % seg_sub == 0:
                    ts("dve", Bcol.ap[:], Bcol.ap[:], V(VI_NFLAG + f0 + seg + 1), ALU.mult,
                       rd=Bcol.all + vec.all, wr=Bcol.all)
                    for h in range(4):
                        ts("pool", Sacc.ap[:, h, :], Sacc.ap[:, h, :], V(VI_NFLAG + f0 + seg + 1), ALU.mult,
                           rd=[Sacc[h]] + vec.all, wr=[Sacc[h]], s2=1.0, op1=ALU.mult)

            stageA(0)
            tick()
            for s in range(SUB):
                stageB1(s)
                if s < SUB - 1:
                    tick()
                if s + 1 < SUB:
                    stageA(s + 1)
                    tick()
                if s == SUB - 1:
                    kv_section()
                    pump(KC)
                    if t + 1 < nt:
                        S.tag = 'P1.x2'
                        x_part2(ssacc[(t + 1) % 2], PB[7])
                stageB2(s)
                if s < SUB - 1:
                    tick()
            pump_cast(cq[0])
            if t == nt_gla - 1:
                for h in range(4):
                    stt(accf.ap[:, h, :], Sf.ap[:, h, :], V(VI_FLAG + f0 + nseg - 1), accf.ap[:, h, :], ALU.mult,
                        ALU.add, rd=[Sf[h], accf[h]] + vec.all, wr=[accf[h]])
    pump_cast(10000)

    S.tag = "P2"
    for seq, o0, olen in (("P", 0, OP), ("S", OP, OS)):
        for h in range(4):
            cpy("pool", Sf.ap[:, h, :], Sbin[seq].ap[:, h, :], rd=[Sbin[seq][h]], wr=[Sf[h]])
        tl = list(reversed(range(olen // 512)))
        for _ in x_part1(xo, o0 + tl[0] * 512, VI_GMIX, xgs[0], ssacc[0], pe_acc=PB[7]):
            pass
        for ti, t in enumerate(tl):
            t0 = o0 + t * 512
            cur["xg"] = xgs[ti % 2]
            x_part2(ssacc[ti % 2], PB[7])
            nxt = (x_part1(xo, o0 + tl[ti + 1] * 512, VI_GMIX, xgs[(ti + 1) % 2], ssacc[(ti + 1) % 2],
                           pe_acc=PB[7]) if ti + 1 < len(tl) else iter(()))
            lr_proj(wlr)
            sl = list(reversed(range(SUB)))

            def stageA2(s):
                b = s % 2
                gla_kv(s, b)
                softplus_neg(1, s, sp[b][1], e_t)

            stageA2(sl[0])
            for si, s in enumerate(sl):
                b = s % 2
                gs = (t0 // 128) + s
                kd_make(1, b, CI_UB, False)
                decay_make(1, b)
                if si + 1 < SUB:
                    stageA2(sl[si + 1])
                for h in range(4):
                    cpy("act" if h % 2 else "pool", sbf.ap[:, h, :], Sf.ap[:, h, :], rd=[Sf[h]], wr=[sbf[h]])
                st(SbScr[gs], sbf.ap[:], rd=sbf.all)
                for h in range(4):
                    pU = state_U(1, h, b)
                    stt(Sf.ap[:, h, :], Sf.ap[:, h, :], dec.ap[:, 2 * h:2 * h + 1], pU.ap[:, 0:256], ALU.mult,
                        ALU.add, rd=[Sf[h]] + dec.all + pU.all, wr=[Sf[h]])
                for _ in range(4):
                    next(nxt, None)
            for _ in nxt:
                pass
    cur["xg"] = xg
    S.barrier()
    sb.reset(m1)
    wv_ap = wb_ap.rearrange("(k p) c -> p k c", p=128)
    wv_gp = wb_gp.rearrange("(k p) c -> p k c", p=128)
    wv_out = wb_out.rearrange("(k p) c -> p k c", p=128)
    wv_up = wb_up.rearrange("(k p) c -> p k c", p=128)
    wv_dn = wb_down.rearrange("(k p) c -> p k c", p=128)
    wspec = {"qa0": (wv_in, 0, KC, C_QA, 512), "qa1": (wv_in, 0, KC, C_QA + 512, 512),
             "qg": (wv_in, 0, KC, C_QG, 512), "kg": (wv_in, 0, KC, C_KG, 512),
             "vg0": (wv_in, 0, KC, C_VG, 512), "vg1": (wv_in, 0, KC, C_VG + 512, 512),
             "gg0": (wv_in, 0, KC, C_GG, 512), "gg1": (wv_in, 0, KC, C_GG + 512, 512),
             "lr": (wv_in, 0, KC, C_LRF, 32)}
    wseq1 = ["qa0", "qa1", "qg", "kg", "lr", "vg0", "vg1", "gg0", "gg1"]
    for cg in range(4):
        wspec["ga%d" % cg] = (wv_in, 0, KC, C_GA + 512 * cg, 512)
        wspec["gb%d" % cg] = (wv_in, 0, KC, C_GB + 512 * cg, 512)
        wspec["ap%d" % cg] = (wv_ap, 0, 8, 512 * cg, 512)
        wspec["gp%d" % cg] = (wv_gp, 0, 8, 512 * cg, 512)
        wseq1 += ["ga%d" % cg, "gb%d" % cg, "ap%d" % cg, "gp%d" % cg]
    for cg in range(4):
        wspec["wo%d" % cg] = (wv_out, 0, KC, 512 * cg, 512)
        wseq1.append("wo%d" % cg)
    for G in range(4):
        for j in range(4):
            wspec["up%d_%d" % (G, j)] = (wv_up, 0, KC, G * 2048 + 512 * j, 512)
            wseq1.append("up%d_%d" % (G, j))
        for cg in range(4):
            wspec["dn%d_%d" % (G, cg)] = (wv_dn, G * KC, KC, 512 * cg, 512)
            wseq1.append("dn%d_%d" % (G, cg))
    NMT = OWN // MT
    wseq = wseq1 * NMT
    NW = 2
    wring = [sb.alloc([128, KC, 512], BF16) for _ in range(NW)]
    wstate = {"issued": 0, "pos": 0}

    def w_issue(upto):
        while wstate["issued"] <= min(upto, len(wseq) - 1):
            p = wstate["issued"]
            view, k0, nk, c0, ncol = wspec[wseq[p]]
            slot = wring[p % NW]
            ld(slot.ap[:, 0:nk, 0:ncol], view[:, k0:k0 + nk, c0:c0 + ncol], wr=slot.all)
            wstate["issued"] += 1

    def wnext(name):
        p = wstate["pos"]
        assert wseq[p] == name, (wseq[p], name)
        w_issue(p + NW - 1)
        wstate["pos"] += 1
        return wring[p % NW]

    cs_own = sb.alloc([128, 2, 512], F32)
    mf4 = sb.alloc([128, 4, 128], F32)
    mb4 = sb.alloc([128, 4, 128], F32)
    Sf_bf = sb.alloc([128, 4, 256], BF16, 4)
    for h in range(4):
        cpy("dve", mf4.ap[:, h, :], cst.ap[:, CI_MF:CI_MF + 128], rd=cst.all, wr=mf4.all)
        cpy("pool", mb4.ap[:, h, :], cst.ap[:, CI_MB:CI_MB + 128], rd=cst.all, wr=mb4.all)
    attnO = sb.alloc([128, 8, 512], BF16, 8)
    glaO = sb.alloc([128, 8, 512], BF16, 8)
    zm = sb.mark()
    xov = xo.rearrange("(k p) t -> p k t", p=128)
    PO2 = pbig[3]
    PO2_trk = PB[6].all + PB[7].all

    for m in range(NMT):
        tok0 = m * MT
        seq = "P" if tok0 < OP else "S"
        ctx = TP if seq == "P" else TS
        first_of_seg = (tok0 == 0) or (tok0 == OP)
        S.tag = "P3.x"
        sb.cur = zm + 36 * 1024
        bigring = xring + [sb.alloc([128, 512], F32) for _ in range(KC - 3)]
        x_tile(xo, tok0, VI_GMIX, bigring)
        ld(cs_own.ap[:], cs_o[:, :, tok0:tok0 + 512], wr=cs_own.all)

        S.tag = "P3.attnq"
        sb.reset(zm)
        qT = sb.alloc([128, 4, 512], BF16, 4)
        kring = [sb.alloc([128, 512], BF16) for _ in range(3)]
        vring = [sb.alloc([128, SUB, 128], BF16) for _ in range(3)]
        ptring = [sb.alloc([128, 512], BF16) for _ in range(4)]
        rinv = sb.alloc([128, 512], F32)
        sacc = [sb.alloc([128, 512], F32) for _ in range(2)]
        del nr_sets[1:]
        nr_sets.append([sb.alloc([128, 512], F32) for _ in range(6)] + [sb.alloc([128, 512], BF16)])
        for g in range(2):
            S.tag = "P3.attnq"
            w = wnext("qa%d" % g)
            for hh in range(4):
                pq = nb()
                for k in range(KC):
                    mm(pq.ap[:], w.ap[:, k, hh * 128:(hh + 1) * 128], xg.ap[:, k, :], k == 0, k == KC - 1,
                       rd=w.all + [xg[k]], wr=pq.all)
                qk_nr1(pq, 0, hh % 2)
                if hh >= 1:
                    qk_nr2(0, (hh - 1) % 2, cs_own, qT.ap[:, hh - 1, :], [qT[hh - 1]])
            qk_nr2(0, 1, cs_own, qT.ap[:, 3, :], [qT[3]])
            S.tag = "P3.attnq2"
            nkt = ctx // 512
            for pair in ((0, 1), (2, 3)):
                def kv_load(kt):
                    kr, vr = kring[kt % 3], vring[kt % 3]
                    ld(kr.ap[:], Kt[seq][g, :, kt * 512:(kt + 1) * 512], wr=kr.all)
                    ld(vr.ap[:], Vs[seq][kt * 512:(kt + 1) * 512, g * 128:(g + 1) * 128].rearrange(
                        "(s p) c -> p s c", p=128), wr=vr.all)
                S.tag = "P3.attn"
                kv_load(0)
                steps = [(kt, s) for kt in range(nkt) for s in range(SUB)]
                nst = len(steps)

                def qk(i):
                    kt, s = steps[i]
                    kr = kring[kt % 3]
                    for idx, hh in enumerate(pair):
                        bs = PB[2 * idx + (i % 2)]
                        mm(bs.ap[:], kr.ap[:, s * 128:(s + 1) * 128], qT.ap[:, hh, :], True, True,
                           rd=kr.all + [qT[hh]], wr=bs.all)

                qk(0)
                for i in range(nst):
                    kt, s = steps[i]
                    if s == 0 and kt + 1 < nkt:
                        kv_load(kt + 1)
                    if i + 1 < nst:
                        qk(i + 1)
                    vr = vring[kt % 3]
                    for idx, hh in enumerate(pair):
                        bs = PB[2 * idx + (i % 2)]
                        pt = ptring[(2 * i + idx) % 4]
                        act(pt.ap[:], bs.ap[:], AF.Exp, rd=bs.all + negb.all, wr=pt.all, bias=negb.ap[:, 0:1])
                        mm(PB[4 + idx].ap[:], vr.ap[:, s, :], pt.ap[:], i == 0, i == nst - 1,
                           rd=vr.all + pt.all, wr=PB[4 + idx].all)
                        if i == 0:
                            cpy("dve", sacc[idx].ap[:], pt.ap[:], rd=pt.all, wr=sacc[idx].all)
                        else:
                            tt("dve", sacc[idx].ap[:], sacc[idx].ap[:], pt.ap[:], ALU.add,
                               rd=sacc[idx].all + pt.all, wr=sacc[idx].all)
                S.tag = "P3.attnfin"
                for idx, hh in enumerate(pair):
                    mm(PB[6 + idx].ap[:], C(CI_ONES), sacc[idx].ap[:], True, True,
                       rd=cst.all + sacc[idx].all, wr=PB[6 + idx].all)
                    act(sacc[idx].ap[:], PB[6 + idx].ap[:], AF.Ln, rd=PB[6 + idx].all, wr=sacc[idx].all)
                    act(rinv.ap[:], sacc[idx].ap[:], AF.Exp, rd=sacc[idx].all, wr=rinv.all, scale=-1.0)
                    tt("dve", attnO.ap[:, g * 4 + hh, :], PB[4 + idx].ap[:], rinv.ap[:], ALU.mult,
                       rd=PB[4 + idx].all + rinv.all, wr=[attnO[g * 4 + hh]])
        S.barrier()

        S.tag = "P3.glaproj"
        sb.reset(zm)
        qgT = sb.alloc([128, 4, 512], BF16)
        kgT = sb.alloc([128, 4, 512], BF16)
        kgm = sb.alloc([128, SUB, 512], BF16, SUB)
        vgm = sb.alloc([128, SUB, 1024], BF16, SUB)
        sgn = sb.alloc([128, 8, 512], BF16)
        t8 = sb.alloc([128, 8, 128], F32)
        gt1 = T(t8.ap[:, 0:4, :].rearrange("p a t -> p (a t)"))
        gt1.c = t8.c
        gt2 = T(t8.ap[:, 4:8, :].rearrange("p a t -> p (a t)"))
        gt2.c = t8.c
        spm = [[sb.alloc([128, 512], F32) for _ in range(2)] for _ in range(2)]
        e_m = sb.alloc([128, 512], F32)
        Eq = [sb.alloc([128, 4, 128], F32) for _ in range(2)]
        Ek = [sb.alloc([128, 4, 128], F32) for _ in range(2)]
        qe = [sb.alloc([128, 4, 128], BF16) for _ in range(2)]
        ke = [sb.alloc([128, 4, 128], BF16) for _ in range(2)]
        Am = [sb.alloc([128, 4, 128], BF16) for _ in range(2)]
        ekdm = sb.alloc([128, 512], F32)
        kdm = sb.alloc([128, 512], BF16)
        sbsn = [sb.alloc([128, 4, 256], BF16) for _ in range(2)]
        osq = sb.alloc([128, 8, 128], BF16)
        rso = sb.alloc([128, 4, 128], F32)
        rro = sb.alloc([128, 4, 128], F32)

        w = wnext("qg")
        for h in range(4):
            pq = nb()
            for k in range(KC):
                mm(pq.ap[:], w.ap[:, k, h * 128:(h + 1) * 128], xg.ap[:, k, :], k == 0, k == KC - 1,
                   rd=w.all + [xg[k]], wr=pq.all)
            stt(qgT.ap[:, h, :], pq.ap[:], 128.0 ** -0.5, rstd_b.ap[:], ALU.mult, ALU.mult,
                rd=pq.all + rstd_b.all, wr=qgT.all)
        w = wnext("kg")
        for h in range(4):
            pq = nb()
            for k in range(KC):
                mm(pq.ap[:], w.ap[:, k, h * 128:(h + 1) * 128], xg.ap[:, k, :], k == 0, k == KC - 1,
                   rd=w.all + [xg[k]], wr=pq.all)
            tt("dve", kgT.ap[:, h, :], pq.ap[:], rstd_b.ap[:], ALU.mult, rd=pq.all + rstd_b.all, wr=kgT.all)
        for s in range(SUB):
            pk = tok_proj(w, s, slice(0, 512))
            act(kgm.ap[:, s, :], pk.ap[:], AF.Copy, rd=pk.all + rcol.all, wr=[kgm[s]],
                scale=rcol.ap[:, 2 * s:2 * s + 1])
        w = wnext("lr")
        lr_proj(w)
        softplus_neg(0, 0, spm[0][0], e_m)
        softplus_neg(1, 0, spm[0][1], e_m)
        for i in range(2):
            w = wnext("vg%d" % i)
            for s in range(SUB):
                pv = tok_proj(w, s, slice(0, 512))
                ts("dve", vgm.ap[:, s, 512 * i:512 * (i + 1)], pv.ap[:], rcol.ap[:, 2 * s:2 * s + 1], ALU.mult,
                   rd=pv.all + rcol.all, wr=[vgm[s]])
        for i in range(2):
            w = wnext("gg%d" % i)
            for c4 in range(4):
                ch = 4 * i + c4
                pg = nb()
                for k in range(KC):
                    mm(pg.ap[:], w.ap[:, k, c4 * 128:(c4 + 1) * 128], xg.ap[:, k, :], k == 0, k == KC - 1,
                       rd=w.all + [xg[k]], wr=pg.all)
                tt("dve", gt1.ap[:], pg.ap[:], rstd_b.ap[:], ALU.mult, rd=pg.all + rstd_b.all, wr=gt1.all)
                act(gt2.ap[:], gt1.ap[:], AF.Silu, rd=gt1.all, wr=gt2.all)
                ts("pool", sgn.ap[:, ch, :], gt2.ap[:], V(VI_GN + ch % 2), ALU.mult, rd=gt2.all + vec.all,
                   wr=sgn.all, s2=1.0, op1=ALU.mult)
        if first_of_seg:
            for h in range(4):
                cpy("pool", Sf.ap[:, h, :], Sfin[seq].ap[:, h, :], rd=[Sfin[seq][h]], wr=[Sf[h]])
                cpy("act", Sf_bf.ap[:, h, :], Sf.ap[:, h, :], rd=[Sf[h]], wr=[Sf_bf[h]])
        S.tag = "P3.gla"
        for s in range(SUB):
            gs = m * SUB + s
            sn = sbsn[s % 2]
            ld(sn.ap[:], SbScr[gs], wr=sn.all)
            spc = spm[s % 2]
            pbts = []
            for d_ in range(2):
                pbt = nb()
                for h in range(4):
                    mm(pbt.ap[:, h * 128:(h + 1) * 128], spc[d_].ap[:, h * 128:(h + 1) * 128],
                       C(CI_LF if d_ == 0 else CI_LB), True, True, rd=spc[d_].all + cst.all, wr=pbt.all)
                pbts.append(pbt)
            pu = nb()
            mm(pu.ap[:], C(CI_UF), spc[0].ap[:], True, True, rd=cst.all + spc[0].all, wr=pu.all)
            if s + 1 < SUB:
                softplus_neg(0, s + 1, spm[(s + 1) % 2][0], e_m)
                softplus_neg(1, s + 1, spm[(s + 1) % 2][1], e_m)
            for d_ in range(2):
                pbt = pbts[d_]
                pbv = pbt.ap[:].rearrange("p (h t) -> p h t", h=4)
                act(Eq[d_].ap[:], pbv, AF.Exp, rd=pbt.all, wr=Eq[d_].all)
                act(Ek[d_].ap[:], pbv, AF.Exp, rd=pbt.all, wr=Ek[d_].all, scale=-1.0)
                tt("dve", qe[d_].ap[:], qgT.ap[:, :, s * 128:(s + 1) * 128], Eq[d_].ap[:], ALU.mult,
                   rd=qgT.all + Eq[d_].all, wr=qe[d_].all)
                tt("pool", ke[d_].ap[:], kgT.ap[:, :, s * 128:(s + 1) * 128], Ek[d_].ap[:], ALU.mult,
                   rd=kgT.all + Ek[d_].all, wr=ke[d_].all)
            act(ekdm.ap[:], pu.ap[:], AF.Exp, rd=pu.all, wr=ekdm.all)
            tt("pool", kdm.ap[:], kgm.ap[:, s, :], ekdm.ap[:], ALU.mult, rd=[kgm[s]] + ekdm.all, wr=kdm.all)
            for d_ in range(2):
                pA = nb()
                for h in range(4):
                    mm(pA.ap[:, h * 128:(h + 1) * 128], ke[d_].ap[:, h, :], qe[d_].ap[:, h, :], True, True,
                       rd=ke[d_].all + qe[d_].all, wr=pA.all)
                msk = mf4 if d_ == 0 else mb4
                tt("dve", Am[d_].ap[:], pA.ap[:].rearrange("p (h t) -> p h t", h=4), msk.ap[:], ALU.mult,
                   rd=pA.all + msk.all, wr=Am[d_].all)
            for h in range(4):
                for c in range(2):
                    o_ap = PO2[:, (h * 2 + c) * 128:(h * 2 + c + 1) * 128]
                    vl = vgm.ap[:, s, h * 256 + c * 128:h * 256 + (c + 1) * 128]
                    mm(o_ap, vl, Am[0].ap[:, h, :], True, False, rd=[vgm[s]] + Am[0].all, wr=PO2_trk)
                    mm(o_ap, Sf_bf.ap[:, h, c * 128:(c + 1) * 128], qe[0].ap[:, h, :], False, False,
                       rd=[Sf_bf[h]] + qe[0].all, wr=PO2_trk)
                    mm(o_ap, vl, Am[1].ap[:, h, :], False, False, rd=[vgm[s]] + Am[1].all, wr=PO2_trk)
                    mm(o_ap, sn.ap[:, h, c * 128:(c + 1) * 128], qe[1].ap[:, h, :], False, True,
                       rd=sn.all + qe[1].all, wr=PO2_trk)
            for h in range(4):
                pU = nb()
                mm(pU.ap[:, 0:256], kdm.ap[:, h * 128:(h + 1) * 128], vgm.ap[:, s, h * 256:(h + 1) * 256], True, True,
                   rd=kdm.all + [vgm[s]], wr=pU.all)
                stt(Sf.ap[:, h, :], Sf.ap[:, h, :], Eq[0].ap[:, h, 127:128], pU.ap[:, 0:256], ALU.mult, ALU.add,
                    rd=[Sf[h]] + Eq[0].all + pU.all, wr=[Sf[h]])
                cpy("act", Sf_bf.ap[:, h, :], Sf.ap[:, h, :], rd=[Sf[h]], wr=[Sf_bf[h]])
            o8 = PO2[:, :].rearrange("p (a t) -> p a t", a=8)
            for b2 in range(2):
                act(osq.ap[:, 4 * b2:4 * b2 + 4, :], o8[:, 4 * b2:4 * b2 + 4, :], AF.Square,
                    rd=PO2_trk, wr=osq.all)
            pss = nb()
            for h in range(4):
                for c in range(2):
                    mm(pss.ap[:, h * 128:(h + 1) * 128], ones_bf.ap[:], osq.ap[:, h * 2 + c, :], c == 0, c == 1,
                       rd=ones_bf.all + osq.all, wr=pss.all)
            act(rso.ap[:], pss.ap[:].rearrange("p (h t) -> p h t", h=4), AF.Ln, rd=pss.all, wr=rso.all,
                scale=1.0 / 256, bias=EPS)
            act(rro.ap[:], rso.ap[:], AF.Exp, rd=rso.all, wr=rro.all, scale=-0.5)
            for b2 in range(2):
                tt("dve", t8.ap[:, 4 * b2:4 * b2 + 4, :], o8[:, 4 * b2:4 * b2 + 4, :],
                   sgn.ap[:, 4 * b2:4 * b2 + 4, s * 128:(s + 1) * 128], ALU.mult, rd=PO2_trk + sgn.all, wr=t8.all)
            tt("pool", glaO.ap[:, :, s * 128:(s + 1) * 128].rearrange("p (h c) t -> p h c t", c=2),
               t8.ap[:].rearrange("p (h c) t -> p h c t", c=2),
               rro.ap[:].unsqueeze(2).broadcast_to([128, 4, 2, 128]), ALU.mult,
               rd=t8.all + rro.all, wr=glaO.all)
        S.barrier()

        S.tag = "P3.merge"
        sb.reset(zm)
        sigA = sb.alloc([128, 4, 512], F32)
        sigB = sb.alloc([128, 4, 512], F32)
        part = sb.alloc([128, 4, 512], F32)
        mt1 = sb.alloc([128, 512], F32)
        sb.cur = zm + 32768
        mixed = sb.alloc([128, KC, 512], BF16, KC)
        for cg in range(4):
            for nm, sg_t, bcol in (("ga", sigA, VI_BMA), ("gb", sigB, VI_BMB)):
                w = wnext("%s%d" % (nm, cg))
                for c4 in range(4):
                    pg = nb()
                    for k in range(KC):
                        mm(pg.ap[:], w.ap[:, k, c4 * 128:(c4 + 1) * 128], xg.ap[:, k, :], k == 0, k == KC - 1,
                           rd=w.all + [xg[k]], wr=pg.all)
                    tt("dve", mt1.ap[:], pg.ap[:], rstd_b.ap[:], ALU.mult, rd=pg.all + rstd_b.all, wr=mt1.all)
                    act(sg_t.ap[:, c4, :], mt1.ap[:], AF.Sigmoid, rd=mt1.all + vec.all, wr=sg_t.all,
                        bias=V(bcol + cg * 4 + c4))
            w = wnext("ap%d" % cg)
            for c4 in range(4):
                pa = nb()
                for k in range(8):
                    mm(pa.ap[:], w.ap[:, k, c4 * 128:(c4 + 1) * 128], attnO.ap[:, k, :], k == 0, k == 7,
                       rd=w.all + [attnO[k]], wr=pa.all)
                tt("dve", part.ap[:, c4, :], pa.ap[:], sigA.ap[:, c4, :], ALU.mult, rd=pa.all + sigA.all,
                   wr=part.all)
            w = wnext("gp%d" % cg)
            for c4 in range(4):
                pa = nb()
                for k in range(8):
                    mm(pa.ap[:], w.ap[:, k, c4 * 128:(c4 + 1) * 128], glaO.ap[:, k, :], k == 0, k == 7,
                       rd=w.all + glaO.all, wr=pa.all)
                tt("dve", mt1.ap[:], pa.ap[:], sigB.ap[:, c4, :], ALU.mult, rd=pa.all + sigB.all, wr=mt1.all)
                tt("pool", mixed.ap[:, cg * 4 + c4, :], mt1.ap[:], part.ap[:, c4, :], ALU.add,
                   rd=mt1.all + part.all, wr=[mixed[cg * 4 + c4]])
        S.barrier()

        S.tag = "P3.wout"
        sb.reset(zm)
        hT = sb.alloc([128, KC, 512], F32, KC)
        sb.cur = zm + 49152
        rtmp = sb.alloc([128, 512], BF16)
        ytile = [sb.alloc([128, 512], F32) for _ in range(2)]
        ss2 = PB[6]
        for cg in range(4):
            w = wnext("wo%d" % cg)
            for c4 in range(4):
                c = cg * 4 + c4
                po = nb()
                for k in range(KC):
                    mm(po.ap[:], w.ap[:, k, c4 * 128:(c4 + 1) * 128], mixed.ap[:, k, :], k == 0, k == KC - 1,
                       rd=w.all + [mixed[k]], wr=po.all)
                if c == 0:
                    for c_ in range(2):
                        ld(xring[c_ % 3].ap[:], xov[:, c_, tok0:tok0 + 512], wr=xring[c_ % 3].all)
                if c + 2 < KC:
                    ld(xring[(c + 2) % 3].ap[:], xov[:, c + 2, tok0:tok0 + 512], wr=xring[(c + 2) % 3].all)
                xr = xring[c % 3]
                tt("dve", hT.ap[:, c, :], po.ap[:], xr.ap[:], ALU.add, rd=po.all + xr.all, wr=[hT[c]])
                if c >= 1:
                    sqp = sqring[(c - 1) % 2]
                    mm(ss2.ap[:], ones_bf.ap[:], sqp.ap[:], c == 1, False, rd=ones_bf.all + sqp.all, wr=ss2.all)
                sq = sqring[c % 2]
                act(sq.ap[:], hT.ap[:, c, :], AF.Square, rd=[hT[c]], wr=sq.all)
        sqp = sqring[(KC - 1) % 2]
        mm(ss2.ap[:], ones_bf.ap[:], sqp.ap[:], False, True, rd=ones_bf.all + sqp.all, wr=ss2.all)
        mk_rstd(ss2, 1.0 / D, rstd_b)
        for c in range(KC):
            stt(xg.ap[:, c, :], hT.ap[:, c, :], V(VI_GMLP + c), rstd_b.ap[:], ALU.mult, ALU.mult,
                rd=[hT[c]] + vec.all + rstd_b.all, wr=[xg[c]])

        S.tag = "P3.mlp"
        sb.cur = zm + 32768
        uG = sb.alloc([128, KC, 512], BF16, KC)
        uG.c = mixed.c
        for G in range(4):
            for j in range(4):
                w = wnext("up%d_%d" % (G, j))
                for c4 in range(4):
                    f = j * 4 + c4
                    pu = nb()
                    for k in range(KC):
                        mm(pu.ap[:], w.ap[:, k, c4 * 128:(c4 + 1) * 128], xg.ap[:, k, :], k == 0, k == KC - 1,
                           rd=w.all + [xg[k]], wr=pu.all)
                    act(rtmp.ap[:], pu.ap[:], AF.Relu, rd=pu.all, wr=rtmp.all)
                    tt("pool", uG.ap[:, f, :], rtmp.ap[:], rtmp.ap[:], ALU.mult, rd=rtmp.all, wr=[uG[f]])
            for cg in range(4):
                w = wnext("dn%d_%d" % (G, cg))
                for c4 in range(4):
                    c = cg * 4 + c4
                    pd = nb()
                    for f in range(KC):
                        mm(pd.ap[:], w.ap[:, f, c4 * 128:(c4 + 1) * 128], uG.ap[:, f, :], f == 0, f == KC - 1,
                           rd=w.all + [uG[f]], wr=pd.all)
                    tt("dve", hT.ap[:, c, :], hT.ap[:, c, :], pd.ap[:], ALU.add, rd=[hT[c]] + pd.all, wr=[hT[c]])
        S.tag = "P3.final"
        for c in range(KC):
            sq = sqring[c % 2]
            act(sq.ap[:], hT.ap[:, c, :], AF.Square, rd=[hT[c]], wr=sq.all)
            mm(ss2.ap[:], ones_bf.ap[:], sq.ap[:], c == 0, c == KC - 1, rd=ones_bf.all + sq.all, wr=ss2.all)
        mk_rstd(ss2, 1.0 / D, rstd_b)
        for c in range(KC):
            yt = ytile[c % 2]
            stt(yt.ap[:], hT.ap[:, c, :], V(VI_GFIN + c), rstd_b.ap[:], ALU.mult, ALU.mult,
                rd=[hT[c]] + vec.all + rstd_b.all, wr=yt.all)
            st(yT[c * 128:(c + 1) * 128, tok0:tok0 + 512], yt.ap[:], rd=yt.all, is_out=True)
        S.barrier()

    S.emit(sems, dsems)
    nc._tagnames = S.names
    print("SBUF peak", sb.peak, "of", sb.top, " instrs:", {n: len(S.q[n].ins) for n in Sched.ENGS})
    return nc


def prep_in_maps(inputs, cfg, cores=range(8)):
    f = np.float32
    TP, TS, OP, OS = cfg.TP, cfg.TS, cfg.OP, cfg.OS
    xpa = np.asarray(inputs["x_prompt"], f)
    xsT = np.ascontiguousarray(np.asarray(inputs["x_sample"], f)[0].T)
    consts = host_consts()
    cs = host_rope(TS)
    g = lambda k: np.asarray(inputs[k], f)
    z16 = np.zeros((16, 512), f)
    wgu = np.stack([np.concatenate([g("w_gate_up_fwd")[0], z16, g("b_gate_fwd")[0][None]], 0),
                    np.concatenate([z16, g("w_gate_up_bwd")[0], g("b_gate_bwd")[0][None]], 0)], 0)
    rows = np.concatenate([g("q_norm")[0], g("k_norm")[0]])[None, :]
    base = np.zeros((128, NV), f)
    base[:, VI_GMIX:VI_GMIX + 16] = g("norm_mix")[0].reshape(16, 128).T
    base[:, VI_GMLP:VI_GMLP + 16] = g("norm_mlp")[0].reshape(16, 128).T
    base[:, VI_GFIN:VI_GFIN + 16] = g("norm_final").reshape(16, 128).T
    base[:, VI_GQ] = g("q_norm")[0]
    base[:, VI_GK] = g("k_norm")[0]
    base[:, VI_GN:VI_GN + 2] = g("gla_norm")[0].reshape(2, 128).T
    base[:, VI_BMA:VI_BMA + 16] = g("b_merge")[0][:D].reshape(16, 128).T
    base[:, VI_BMB:VI_BMB + 16] = g("b_merge")[0][D:].reshape(16, 128).T
    base[:, VI_NFLAG:VI_NFLAG + 10] = 1.0
    base[:, VI_NEG16:VI_NEG16 + 2] = -1.0 / 16
    base[:, VI_ONE:VI_ONE + 2] = 1.0
    shared = {"consts": consts, "rows": np.ascontiguousarray(rows), "wgu": wgu,
              "w_in": g("w_in")[0], "w_ap": g("w_attn_proj")[0], "w_gp": g("w_gla_proj")[0],
              "w_out": g("w_out")[0], "w_up": g("w_up")[0], "w_down": g("w_down")[0]}
    maps = []
    xpT_cache = {}
    for c in cores:
        b, h = c // 2, c % 2
        if b not in xpT_cache:
            xpT_cache[b] = np.ascontiguousarray(xpa[b].T)
        xpT = xpT_cache[b]
        vecs = base.copy()
        vecs[:, VI_FLAG + h] = 1.0
        vecs[:, VI_NFLAG + h] = 0.0
        vecs[:, VI_FLAG + 2 + c] = 1.0
        vecs[:, VI_NFLAG + 2 + c] = 0.0
        m = dict(shared)
        po = [s_ for s_ in range(NSEG_P) if s_ != h] + [h]
        so = [s_ for s_ in range(NSEG_S) if s_ != c] + [c]
        m["xp"] = np.ascontiguousarray(np.concatenate([xpT[:, s_ * OP:(s_ + 1) * OP] for s_ in po], 1))
        m["xs"] = np.ascontiguousarray(np.concatenate([xsT[:, s_ * OS:(s_ + 1) * OS] for s_ in so], 1))
        m["cs_cp"] = np.ascontiguousarray(np.concatenate([cs[:, :, s_ * OP:(s_ + 1) * OP] for s_ in po], 2))
        m["cs_cs"] = np.ascontiguousarray(np.concatenate([cs[:, :, s_ * OS:(s_ + 1) * OS] for s_ in so], 2))
        m["xo"] = np.ascontiguousarray(np.concatenate([xpT[:, h * OP:(h + 1) * OP], xsT[:, c * OS:(c + 1) * OS]], 1))
        m["cs_o"] = np.ascontiguousarray(np.concatenate([cs[:, :, h * OP:(h + 1) * OP],
                                                          cs[:, :, c * OS:(c + 1) * OS]], 2))
        m["vecs"] = vecs
        maps.append(m)
    return maps


def assemble(results, cfg, cores=range(8)):
    TP, TS, OP, OS = cfg.TP, cfg.TS, cfg.OP, cfg.OS
    yp = np.zeros((4, TP, D), np.float32)
    ys = np.zeros((1, TS, D), np.float32)
    for r, c in zip(results, cores):
        b, h = c // 2, c % 2
        y = np.asarray(r["yT"])
        yp[b, h * OP:(h + 1) * OP, :] = y[:, :OP].T
        ys[0, c * OS:(c + 1) * OS, :] = y[:, OP:].T
    return yp, ys


def kernel(**inputs):
    cfg = Cfg()
    nc = build(cfg)
    maps = prep_in_maps(inputs, cfg)
    res = run_bass_kernel_spmd(nc, maps, core_ids=list(range(8)))
    return assemble(res.results, cfg)
```
